# Optimizing a Trainium2 kernel written in Bass

```python
import jax, jax.numpy as jnp
from jax import lax
import numpy as np

D_MODEL = 2048
BATCH = 2
SEQ = 8192
DEPTH = 2

RMS_EPS = 1e-6
ROPE_THETA = 10000.0
D_FF = 5632
Q_BLOCK = 128

MLA_HEADS = 8
MLA_Q_LORA = 512
MLA_KV_LORA = 256
MLA_NOPE_DIM = 128
MLA_ROPE_DIM = 64
MLA_V_DIM = 128

SWA_HEADS = 8
SWA_KV_HEADS = 2
SWA_HEAD_DIM = 64
WINDOW = 128

FOX_HEADS = 8
FOX_HEAD_DIM = 64

IN_SPLITS = [
    MLA_Q_LORA,
    MLA_KV_LORA,
    MLA_ROPE_DIM,
    SWA_HEADS * SWA_HEAD_DIM,
    SWA_KV_HEADS * SWA_HEAD_DIM,
    SWA_KV_HEADS * SWA_HEAD_DIM,
    FOX_HEADS * FOX_HEAD_DIM,
    FOX_HEADS * FOX_HEAD_DIM,
    FOX_HEADS * FOX_HEAD_DIM,
    FOX_HEADS,
]
IN_COLS = int(sum(IN_SPLITS))
IN_OFFSETS = [int(o) for o in np.cumsum(IN_SPLITS)[:-1]]
MIX_WIDTH = MLA_HEADS * MLA_V_DIM + SWA_HEADS * SWA_HEAD_DIM + FOX_HEADS * FOX_HEAD_DIM

kernel_name = "hybrid_mla_swa_sink_fox_macaron"


def rmsnorm(x, g):
    x32 = x.astype(jnp.float32)
    y = x32 * lax.rsqrt(jnp.mean(x32 * x32, axis=-1, keepdims=True) + RMS_EPS)
    return (y * g.astype(jnp.float32)).astype(x.dtype)


def swiglu(h, w_gate, w_up, w_down):
    return (jax.nn.silu(h @ w_gate) * (h @ w_up)) @ w_down


def rope_tables(positions, dim):
    inv_freq = ROPE_THETA ** (-jnp.arange(0, dim, 2, dtype=jnp.float32) / dim)
    ang = positions.astype(jnp.float32)[..., None] * inv_freq
    return jnp.cos(ang), jnp.sin(ang)


def apply_rope(x, cos, sin):
    half = x.shape[-1] // 2
    x1, x2 = x[..., :half], x[..., half:]
    c, s = cos[:, :, None, :], sin[:, :, None, :]
    return jnp.concatenate([x1 * c - x2 * s, x2 * c + x1 * s], axis=-1).astype(x.dtype)


def causal_block_attention(q, k, v, scale, log_f_cum=None):
    B, S, H, dq = q.shape
    dv = v.shape[-1]
    n = S // Q_BLOCK
    qb = q.reshape(B, n, Q_BLOCK, H, dq).transpose(1, 0, 2, 3, 4)
    key_pos = jnp.arange(S)
    idx = jnp.arange(n)

    def scores_for(i, qi):
        s = jnp.einsum('bqhd,bkhd->bhqk', qi, k).astype(jnp.float32) * scale
        q_pos = i * Q_BLOCK + jnp.arange(Q_BLOCK)
        return s, key_pos[None, :] <= q_pos[:, None]

    def finish(s, mask):
        s = jnp.where(mask[None, None], s, -jnp.inf)
        p = jax.nn.softmax(s, axis=-1).astype(v.dtype)
        return jnp.einsum('bhqk,bkhd->bqhd', p, v)

    if log_f_cum is None:
        def step(args):
            i, qi = args
            s, mask = scores_for(i, qi)
            return finish(s, mask)
        out = lax.map(step, (idx, qb))
    else:
        c_all = log_f_cum.transpose(0, 2, 1)
        cb = log_f_cum.reshape(B, n, Q_BLOCK, H).transpose(1, 0, 3, 2)

        def step(args):
            i, qi, ci = args
            s, mask = scores_for(i, qi)
            s = s + (ci[..., :, None] - c_all[:, :, None, :])
            return finish(s, mask)
        out = lax.map(step, (idx, qb, cb))
    return out.transpose(1, 0, 2, 3, 4).reshape(B, S, H, dv)


def sliding_window_sink_attention(q, k, v, sinks):
    B, S, H, d = q.shape
    Hkv = k.shape[2]
    G = H // Hkv
    n = S // WINDOW
    qb = q.reshape(B, n, WINDOW, Hkv, G, d)
    pad = jnp.zeros((B, WINDOW, Hkv, d), k.dtype)
    kp = jnp.concatenate([pad, k], axis=1).reshape(B, n + 1, WINDOW, Hkv, d)
    vp = jnp.concatenate([pad.astype(v.dtype), v], axis=1).reshape(B, n + 1, WINDOW, Hkv, d)
    kw = jnp.concatenate([kp[:, :-1], kp[:, 1:]], axis=2)
    vw = jnp.concatenate([vp[:, :-1], vp[:, 1:]], axis=2)
    s = jnp.einsum('bnqhgd,bnkhd->bnhgqk', qb, kw).astype(jnp.float32) * (d ** -0.5)
    blk = jnp.arange(n)[:, None, None]
    q_pos = blk * WINDOW + jnp.arange(WINDOW)[None, :, None]
    k_pos = (blk - 1) * WINDOW + jnp.arange(2 * WINDOW)[None, None, :]
    mask = (k_pos <= q_pos) & (k_pos > q_pos - WINDOW) & (k_pos >= 0)
    s = jnp.where(mask[None, :, None, None], s, -jnp.inf)
    sink = jnp.broadcast_to(sinks.astype(jnp.float32).reshape(Hkv, G)[None, None, :, :, None, None],
                            s.shape[:-1] + (1,))
    p = jax.nn.softmax(jnp.concatenate([s, sink], axis=-1), axis=-1)[..., :-1]
    o = jnp.einsum('bnhgqk,bnkhd->bnqhgd', p.astype(v.dtype), vw)
    return o.reshape(B, S, H * d)


def hybrid_mixer(h, cos_m, sin_m, cos_s, sin_s, w_in, q_norm, w_q_b, kv_norm, w_kv_b,
                 sinks, forget_bias, w_out):
    B, S, _ = h.shape
    proj = h @ w_in
    (c_q, c_kv, k_rope, q_s, k_s, v_s, q_f, k_f, v_f, f_logit) = jnp.split(proj, IN_OFFSETS, axis=-1)

    q = (rmsnorm(c_q, q_norm) @ w_q_b).reshape(B, S, MLA_HEADS, MLA_NOPE_DIM + MLA_ROPE_DIM)
    q_nope, q_pe = q[..., :MLA_NOPE_DIM], apply_rope(q[..., MLA_NOPE_DIM:], cos_m, sin_m)
    kv = (rmsnorm(c_kv, kv_norm) @ w_kv_b).reshape(B, S, MLA_HEADS, MLA_NOPE_DIM + MLA_V_DIM)
    k_nope, v_m = kv[..., :MLA_NOPE_DIM], kv[..., MLA_NOPE_DIM:]
    k_pe = apply_rope(k_rope[:, :, None, :], cos_m, sin_m)
    q_m = jnp.concatenate([q_nope, q_pe], axis=-1)
    k_m = jnp.concatenate([k_nope, jnp.broadcast_to(k_pe, (B, S, MLA_HEADS, MLA_ROPE_DIM))], axis=-1)
    o_mla = causal_block_attention(q_m, k_m, v_m, (MLA_NOPE_DIM + MLA_ROPE_DIM) ** -0.5)

    q_s = apply_rope(q_s.reshape(B, S, SWA_HEADS, SWA_HEAD_DIM), cos_s, sin_s)
    k_s = apply_rope(k_s.reshape(B, S, SWA_KV_HEADS, SWA_HEAD_DIM), cos_s, sin_s)
    v_s = v_s.reshape(B, S, SWA_KV_HEADS, SWA_HEAD_DIM)
    o_swa = sliding_window_sink_attention(q_s, k_s, v_s, sinks)

    log_f = jax.nn.log_sigmoid(f_logit.astype(jnp.float32) + forget_bias.astype(jnp.float32))
    c = jnp.cumsum(log_f, axis=1)
    o_fox = causal_block_attention(q_f.reshape(B, S, FOX_HEADS, FOX_HEAD_DIM),
                                   k_f.reshape(B, S, FOX_HEADS, FOX_HEAD_DIM),
                                   v_f.reshape(B, S, FOX_HEADS, FOX_HEAD_DIM),
                                   FOX_HEAD_DIM ** -0.5, c)

    mixed = jnp.concatenate([o_mla.reshape(B, S, -1), o_swa, o_fox.reshape(B, S, -1)], axis=-1)
    return mixed @ w_out


def setup_inputs(seed: int = 0) -> dict:
    key = jax.random.key(seed)
    ks = jax.random.split(key, 24)
    f32 = jnp.float32

    def w(k, fan_in, fan_out):
        return jax.random.normal(k, (DEPTH, fan_in, fan_out), f32) * fan_in ** -0.5

    def gain(k, dim):
        return 1.0 + 0.1 * jax.random.normal(k, (DEPTH, dim), f32)

    return {
        "x": jax.random.normal(ks[0], (BATCH, SEQ, D_MODEL), f32),
        "positions": jnp.broadcast_to(jnp.arange(SEQ, dtype=jnp.int32), (BATCH, SEQ)),
        "ffn1_norm": gain(ks[1], D_MODEL),
        "ffn1_w_gate": w(ks[2], D_MODEL, D_FF),
        "ffn1_w_up": w(ks[3], D_MODEL, D_FF),
        "ffn1_w_down": w(ks[4], D_FF, D_MODEL),
        "mix_norm": gain(ks[5], D_MODEL),
        "w_in": w(ks[6], D_MODEL, IN_COLS),
        "mla_q_norm": gain(ks[7], MLA_Q_LORA),
        "mla_w_q_b": w(ks[8], MLA_Q_LORA, MLA_HEADS * (MLA_NOPE_DIM + MLA_ROPE_DIM)),
        "mla_kv_norm": gain(ks[9], MLA_KV_LORA),
        "mla_w_kv_b": w(ks[10], MLA_KV_LORA, MLA_HEADS * (MLA_NOPE_DIM + MLA_V_DIM)),
        "swa_sinks": 0.5 * jax.random.normal(ks[11], (DEPTH, SWA_HEADS), f32),
        "fox_forget_bias": 0.1 * jax.random.normal(ks[12], (DEPTH, FOX_HEADS), f32),
        "w_out": w(ks[13], MIX_WIDTH, D_MODEL),
        "ffn2_norm": gain(ks[14], D_MODEL),
        "ffn2_w_gate": w(ks[15], D_MODEL, D_FF),
        "ffn2_w_up": w(ks[16], D_MODEL, D_FF),
        "ffn2_w_down": w(ks[17], D_FF, D_MODEL),
        "final_norm": 1.0 + 0.1 * jax.random.normal(ks[18], (D_MODEL,), f32),
    }


def reference(x, positions, ffn1_norm, ffn1_w_gate, ffn1_w_up, ffn1_w_down, mix_norm, w_in,
              mla_q_norm, mla_w_q_b, mla_kv_norm, mla_w_kv_b, swa_sinks, fox_forget_bias, w_out,
              ffn2_norm, ffn2_w_gate, ffn2_w_up, ffn2_w_down, final_norm):
    cos_m, sin_m = rope_tables(positions, MLA_ROPE_DIM)
    cos_s, sin_s = rope_tables(positions, SWA_HEAD_DIM)
    for l in range(DEPTH):
        x = x + 0.5 * swiglu(rmsnorm(x, ffn1_norm[l]), ffn1_w_gate[l], ffn1_w_up[l], ffn1_w_down[l])
        x = x + hybrid_mixer(rmsnorm(x, mix_norm[l]), cos_m, sin_m, cos_s, sin_s, w_in[l],
                             mla_q_norm[l], mla_w_q_b[l], mla_kv_norm[l], mla_w_kv_b[l],
                             swa_sinks[l], fox_forget_bias[l], w_out[l])
        x = x + 0.5 * swiglu(rmsnorm(x, ffn2_norm[l]), ffn2_w_gate[l], ffn2_w_up[l], ffn2_w_down[l])
    return rmsnorm(x, final_norm)
```

```python
import numpy as np
import ml_dtypes
from contextlib import ExitStack
import concourse.bass as bass
import concourse.mybir as mybir
from concourse.bass_utils import run_bass_kernel_spmd

F32 = mybir.dt.float32
BF16 = mybir.dt.bfloat16
I32 = mybir.dt.int32
AF = mybir.ActivationFunctionType
ALU = mybir.AluOpType

D = 2048
KC = 16
FF = 5632
FC = 44
S = 8192
NTOK = 2048
T = 512
NT = NTOK // T
DEPTH = 2
INC = 3144
EPS = 1e-6
NR = 8
TWO_PI = float(2 * np.pi)
PI = float(np.pi)


class Op:
    __slots__ = ("eng", "fn", "deps", "dma", "chan", "idx", "signal", "val", "bg", "epoch")

    def __init__(self, eng, fn, dma):
        self.eng, self.fn, self.dma = eng, fn, dma
        self.deps = []
        self.chan = None
        self.signal = False
        self.val = 0
        self.bg = False
        self.epoch = 0


class Prog:
    ENGS = ("pe", "act", "dve", "pool", "sp")

    NEPOCH = 8

    def __init__(self, nc, n_dma_chan=14):
        self.nc = nc
        self.epoch = 0
        self.bg_keys = set()
        self.ops = []
        self.last_w = {}
        self.readers = {}
        self.n_dma_chan = n_dma_chan
        self.dma_rr = {e: 0 for e in self.ENGS}
        self.chan_last = {}

    def op(self, eng, fn, reads=(), writes=(), dma=False, bg=False):
        o = Op(eng, fn, dma)
        o.bg = bg
        o.epoch = self.epoch
        o.idx = len(self.ops)
        if bg:
            self.bg_keys.update(writes)
        deps = set()
        for r in reads:
            w = self.last_w.get(r)
            if w is not None:
                deps.add(w)
        for w_ in writes:
            w = self.last_w.get(w_)
            if w is not None:
                deps.add(w)
            for rd in self.readers.get(w_, ()):
                deps.add(rd)
        if eng == "pe":
            deps = {d for d in deps if self.ops[d].eng != "pe"}
        if dma:
            ch = (eng, self.dma_rr[eng] % self.n_dma_chan)
            self.dma_rr[eng] += 1
            o.chan = ch
            prev = self.chan_last.get(ch)
            if prev is not None:
                deps.add(prev)
            self.chan_last[ch] = o.idx
        deps.discard(o.idx)
        o.deps = sorted(deps)
        self.ops.append(o)
        for r in reads:
            self.readers.setdefault(r, []).append(o.idx)
        for w_ in writes:
            self.last_w[w_] = o.idx
            self.readers[w_] = []
        return o.idx

    def barrier(self):
        lasts = {}
        for o in self.ops:
            if o.bg or o.fn is None:
                continue
            lasts[(o.eng, o.chan if o.dma else None)] = o.idx
        tok = sorted(set(lasts.values()))
        self.epoch += 1
        for e in self.ENGS:
            o = Op(e, None, False)
            o.epoch = self.epoch
            o.idx = len(self.ops)
            o.deps = [t for t in tok]
            self.ops.append(o)
        self.last_w = {k: v for k, v in self.last_w.items() if k in self.bg_keys}
        self.readers = {}

    def emit(self, stack):
        nc = self.nc
        ops = self.ops
        for o in ops:
            for d in o.deps:
                ops[d].signal = True
        for o in ops:
            if o.dma:
                o.signal = True
        sems = {}
        for e in self.ENGS:
            for k in range(self.NEPOCH):
                sems[("c", e, k)] = stack.enter_context(nc.semaphore("s_%s%d" % (e, k)))
        for o in ops:
            if o.dma and o.chan not in sems:
                sems[o.chan] = stack.enter_context(nc.semaphore("d_%s_%d" % o.chan))
        cnt = {}
        for o in ops:
            key = o.chan if o.dma else ("c", o.eng, o.epoch % self.NEPOCH)
            if o.signal:
                cnt[key] = cnt.get(key, 0) + (16 if o.dma else 1)
                o.val = cnt[key]
            o.chan = key
        per_eng = {e: [] for e in self.ENGS}
        for o in ops:
            per_eng[o.eng].append(o)
        self.max_vals = cnt
        block = stack.enter_context(nc.Block())
        engmap = {"pe": block.tensor, "act": block.scalar, "dve": block.vector,
                  "pool": block.gpsimd, "sp": block.sync}
        sig_scr = self.sig_scr

        def make(e):
            def body(eng):
                waited = {}
                for o in per_eng[e]:
                    need = {}
                    for d in o.deps:
                        od = ops[d]
                        k = od.chan
                        if od.val > need.get(k, 0):
                            need[k] = od.val
                    for k, v in need.items():
                        if waited.get(k, 0) < v:
                            eng.wait_ge(sems[k], v)
                            waited[k] = v
                    inst = None
                    if o.fn is not None:
                        inst = o.fn(eng)
                    if o.signal:
                        if inst is None:
                            assert e in ("dve", "pool", "act"), e
                            inst = eng.memset(sig_scr[e], 0.0) if e != "act" else eng.activation(
                                out=sig_scr[e], in_=sig_scr[e], func=AF.Copy)
                        inst.then_inc(sems[o.chan], 16 if o.dma else 1)
            return body

        for e in self.ENGS:
            engmap[e](make(e))


_c = [0]


def _alloc(n):
    v = _c[0]
    _c[0] += n
    return v


C_FFN1N = [_alloc(KC) for _ in range(DEPTH)]
C_MIXN = [_alloc(KC) for _ in range(DEPTH)]
C_FFN2N = [_alloc(KC) for _ in range(DEPTH)]
C_FINN = _alloc(KC)
C_QN = [_alloc(4) for _ in range(DEPTH)]
C_KVN = [_alloc(2) for _ in range(DEPTH)]
C_INVF = _alloc(1)
C_SGN = _alloc(1)
C_SINK = [_alloc(4) for _ in range(DEPTH)]
C_FB = [_alloc(1) for _ in range(DEPTH)]
C_OH = _alloc(8)
C_SELA = _alloc(8)
C_SELB = _alloc(8)
C_ID8 = _alloc(8)
NCONST = _c[0]

NMASK = 12
M_TRIL, M_SUP, M_PREV0, M_ID = 8, 9, 10, 11
NEG = -30000.0

O_CQ, O_CKV, O_KR, O_QS, O_KS, O_VS, O_QF, O_KF, O_VF, O_FL = 0, 512, 768, 832, 1344, 1472, 1600, 2112, 2624, 3136
NSW = 704


def build_program(debug=False, nlayers=DEPTH):
    nc = bass.Bass("TRN2", target_bir_lowering=False)
    st = ExitStack()
    P = Prog(nc)

    def din(name, shape, dt):
        return nc.dram_tensor(name, list(shape), dt, kind="ExternalInput").ap()

    def dscr(name, shape, dt):
        return nc.dram_tensor(name, list(shape), dt).ap()

    xT_in = din("xT", [D, NTOK], F32)
    consts = din("consts", [128, NCONST], F32)
    masks_in = din("masks", [128, NMASK * 128], BF16)
    pos_in = din("pos", [1, NTOK], I32)
    outT = nc.dram_tensor("outT", [D, NTOK], F32, kind="ExternalOutput").ap()
    dbg_mixed = dbg_x = None
    if debug:
        dbg_mixed = nc.dram_tensor("dbg_mixed", [D, NTOK], BF16, kind="ExternalOutput").ap()
        dbg_x = nc.dram_tensor("dbg_x", [D, NTOK], F32, kind="ExternalOutput").ap()

    WSPEC = [("wg1", D, FF), ("wu1", D, FF), ("wd1", FF, D), ("win", D, INC), ("winsw", D, NSW),
             ("wqb", 512, 1536), ("wqbsw", 512, 512), ("wkvb", 256, 2048), ("wout", D, D),
             ("wg2", D, FF), ("wu2", D, FF), ("wd2", FF, D)]
    w_in_d, w_sh, w_b = {}, {}, {}
    for nm, R, C in WSPEC:
        w_in_d[nm] = din(nm, [DEPTH, R // NR, C], F32)
        for l in range(DEPTH):
            w_sh[nm, l] = dscr("sh_%s_%d" % (nm, l), [R // NR, C], BF16)
            w_b[nm, l] = dscr("wb_%s_%d" % (nm, l), [R, C], BF16)

    xres = dscr("xres", [D, NTOK], F32)
    QnT_d = [dscr("QnT%d" % l, [1024, NTOK], BF16) for l in range(DEPTH)]
    QpT_d = [dscr("QpT%d" % l, [512, NTOK], BF16) for l in range(DEPTH)]
    Qs_d = [dscr("Qs%d" % l, [512, NTOK], BF16) for l in range(DEPTH)]
    Qf_d = [dscr("Qf%d" % l, [8, 67, NTOK], BF16) for l in range(DEPTH)]
    GSPEC = [("Kn", 1024, NTOK, BF16), ("Kpe", 64, NTOK, BF16), ("Vm", NTOK, 1024, BF16), ("Ks", 128, NTOK, BF16),
             ("Vs", NTOK, 128, BF16), ("Kf", 512, NTOK, BF16), ("Vf", NTOK, 512, BF16), ("FL", 8, NTOK, F32)]
    g_in, g_out = {}, {}
    for nm, R, C, dt in GSPEC:
        for l in range(DEPTH):
            g_in[nm, l] = dscr("gi_%s_%d" % (nm, l), [R, C], dt)
            g_out[nm, l] = dscr("go_%s_%d" % (nm, l), [NR * R, C], dt)

    def sb(name, shape, dt):
        return st.enter_context(nc.sbuf_tensor(name, list(shape), dt))

    cst = sb("cst", [128, NCONST], F32)
    mk = sb("mk", [128, NMASK, 128], BF16)
    xT = sb("xTt", [128, KC, T], F32)
    hT = sb("hT", [128, KC, T], BF16)
    aT = sb("aT", [128, FC, T], BF16)
    wgu = sb("wgu", [128, 4, KC, 256], BF16)
    wd = sb("wd", [128, 2, FC, 256], BF16)
    ones = sb("ones", [128, 128], BF16)
    tmpA = sb("tmpA", [128, 2, T], F32)
    rstd = sb("rstd", [128, T], F32)
    swa_s = sb("swa_s", [128, 6144 // 2 * 2], BF16)
    cpT = sb("cpT", [128, 2, 8, 8, 8], F32)
    small = sb("small", [128, 64], F32)
    sigp = sb("sigp", [128, 8], F32)
    ps = [st.enter_context(nc.psum_tensor("ps%d" % i, [128, 512], F32)) for i in range(8)]
    P.sig_scr = {"dve": sigp[0:1, 0:1], "pool": sigp[0:1, 1:2], "act": sigp[0:1, 2:3]}

    aflat = aT[:].rearrange("p f t -> p (f t)")
    wflat = wd[:].rearrange("p s f c -> p (s f c)")

    def view(flat, byte_off, nelem, dt, pattern=None, **kw):
        if dt == BF16:
            v = flat[:, byte_off // 2: byte_off // 2 + nelem]
        else:
            v = flat[:, byte_off // 2: byte_off // 2 + 2 * nelem].bitcast(dt)
        if pattern:
            v = v.rearrange(pattern, **kw)
        return v

    cq32 = view(aflat, 0, 2048, F32, "p (a t) -> p a t", a=4)
    ckv32 = view(aflat, 8192, 1024, F32, "p (a t) -> p a t", a=2)
    cqn = view(aflat, 12288, 2048, BF16, "p (a t) -> p a t", a=4)
    ckvn = view(aflat, 16384, 1024, BF16, "p (a t) -> p a t", a=2)
    sqs = view(aflat, 18432, 2048, BF16, "p (a t) -> p a t", a=4)
    cos2 = view(aflat, 22528, 512, F32)
    sin2 = view(aflat, 24576, 512, F32)
    ang = view(aflat, 26624, 512, F32)
    angi = view(aflat, 28672, 512, I32)
    posb = view(aflat, 30720, 512, I32)
    stage = view(aflat, 32768, 1024, BF16, "p (a t) -> p a t", a=2)
    stage32 = view(aflat, 34816, 512, F32)
    t1 = view(aflat, 36864, 512, F32)
    t2 = view(aflat, 38912, 512, F32)
    ang2 = view(aflat, 40960, 512, F32)
    FLs = view(aflat, 0, 8192, F32)
    FLs4 = FLs.rearrange("p (m r q) -> p m r q", m=8, r=8)
    cown = view(aflat, 32768, 1024, F32)
    cneg = view(aflat, 36864, 1024, F32)
    KpeAll = view(aflat, 0, 8192, BF16, "p (r t) -> p r t", r=8)
    selG = view(aflat, 16384, 8192, BF16, "p (r t) -> p r t", r=8)
    selacc = view(aflat, 32768, 1024, F32)
    Kc = view(wflat, 0, 3072, BF16, "p (s t) -> p s t", s=3)
    Vc = view(wflat, 6144, 3072, BF16, "p (s m d) -> p s m d", s=3, m=8)
    Qn = view(wflat, 12288, 1024, BF16, "p (s t) -> p s t", s=2)
    Qp = view(wflat, 14336, 1024, BF16, "p (s t) -> p s t", s=2)
    Pb = view(wflat, 16384, 2048, BF16, "p (s t) -> p s t", s=4)
    qs_sb = view(wflat, 20480, 4096, BF16, "p (h t) -> p h t", h=8)
    cq3v = view(wflat, 32768, 3072, BF16, "p (j t) -> p j t", j=3)
    r1 = view(aflat, 40960, 1024, F32)
    Cs = view(wflat, 0, 8192, F32)
    Cs4 = Cs.rearrange("p (m r q) -> p m r q", m=8, r=8)
    wqb_s = view(wflat, 0, 6144, BF16, "p (k c) -> p k c", k=4)
    wqbsw_s = view(wflat, 12288, 2048, BF16, "p (k c) -> p k c", k=4)
    wkvb_s = view(wflat, 16384, 4096, BF16, "p (k c) -> p k c", k=2)
    sflat = swa_s[:]
    KsPrev = sflat[:, 0:2048].rearrange("p (g t) -> p g t", g=2)
    KsOwn = sflat[:, 2048:4096].rearrange("p (g t) -> p g t", g=2)
    VsPrev = sflat[:, 4096:5120].rearrange("p (m d) -> p m d", m=8)
    VsOwn = sflat[:, 5120:6144].rearrange("p (m d) -> p m d", m=8)

    def dma(eng, out, in_, reads=(), writes=()):
        return P.op(eng, lambda e: e.dma_start(out=out, in_=in_), reads=reads, writes=writes, dma=True)

    dma("sp", cst[:], consts, writes=["cst"])
    dma("sp", mk[:].rearrange("p a b -> p (a b)"), masks_in, writes=["mk"])
    P.op("dve", lambda e: e.memset(ones[:], 1.0), writes=["ones"])
    P.op("dve", lambda e: e.memset(sigp[:], 0.0), writes=["sigp"])

    ccsem = st.enter_context(nc.semaphore("ccsem"))
    cc_n = [0]

    def allgather(src, dst, reads, writes, bg=False):
        cc_n[0] += 1
        n = cc_n[0]

        def fn(e):
            e.collective_compute("AllGather", ALU.bypass, replica_groups=[list(range(NR))],
                                 ins=[src.opt()], outs=[dst.opt()]).then_inc(ccsem)
            e.wait_ge(ccsem, n)
            return e.memset(sigp[0:1, 3:4], 0.0)
        P.op("pool", fn, reads=reads, writes=writes, bg=bg)

    def conv_weight(nm, l):
        R, C = [(r, c) for (n_, r, c) in WSPEC if n_ == nm][0]
        src = w_in_d[nm][l]
        dst = w_sh[nm, l]
        c = C
        for cand in (2048, 1572, 1536, 1408, 1024, 704, 512):
            if C % cand == 0 and cand <= 2048:
                c = cand
                break
        s_ap = src.rearrange("a (b c) -> (a b) c", c=c)
        d_ap = dst.rearrange("a (b c) -> (a b) c", c=c)
        P.op("pool", lambda e: e.dma_start(out=d_ap, in_=s_ap), writes=[("sh", nm, l)], dma=True, bg=True)
        allgather(dst, w_b[nm, l], reads=[("sh", nm, l)], writes=[("wb", nm, l)], bg=True)

    def rmsnorm_to_hT(gcol, out_f32=None):
        P.op("act", lambda e: e.activation(out=aT[:, 0:KC, :], in_=xT[:], func=AF.Square),
             reads=["xT"], writes=["aT"] + [("aTc", f) for f in range(KC)])

        def ssmm(e):
            for k in range(KC):
                i = e.matmul(ps[6][:], lhsT=ones[:], rhs=aT[:, k, :], start=(k == 0), stop=(k == KC - 1))
            return i
        P.op("pe", ssmm, reads=["aT", "ones"] + [("aTc", f) for f in range(KC)], writes=[("ps", 6)])
        rstd_from_ss(6, 1.0 / D)
        if out_f32 is None:
            def mkh(e):
                for k in range(KC):
                    i = e.scalar_tensor_tensor(out=hT[:, k, :], in0=xT[:, k, :], scalar=cst[:, gcol + k:gcol + k + 1],
                                               in1=rstd[:], op0=ALU.mult, op1=ALU.mult)
                return i
            P.op("dve", mkh, reads=["xT", "rstd", "cst"], writes=["hT"])
        else:
            def mkf(e):
                for k in range(KC):
                    i = e.scalar_tensor_tensor(out=xT[:, k, :], in0=xT[:, k, :], scalar=cst[:, gcol + k:gcol + k + 1],
                                               in1=rstd[:], op0=ALU.mult, op1=ALU.mult)
                return i
            P.op("dve", mkf, reads=["xT", "rstd", "cst"], writes=["xT"])

    def rstd_from_ss(bank, inv_n):
        P.op("dve", lambda e: e.tensor_scalar(out=tmpA[:, 0, :], in0=ps[bank][:], scalar1=inv_n, scalar2=EPS,
                                              op0=ALU.mult, op1=ALU.add),
             reads=[("ps", bank)], writes=["tmpA0"])
        P.op("dve", lambda e: e.reciprocal(out=tmpA[:, 1, :], in_=tmpA[:, 0, :]), reads=["tmpA0"], writes=["tmpA1"])
        P.op("act", lambda e: e.activation(out=rstd[:], in_=tmpA[:, 1, :], func=AF.Sqrt),
             reads=["tmpA1"], writes=["rstd"])

    wslot = [0]
    dslot = [0]

    def ffn(i, l):
        sfx = "1" if i == 0 else "2"
        wg_d = w_b["wg" + sfx, l].rearrange("(k p) c -> p k c", p=128)
        wu_d = w_b["wu" + sfx, l].rearrange("(k p) c -> p k c", p=128)
        wd_d = w_b["wd" + sfx, l].rearrange("(f p) c -> p f c", p=128)
        for fb in range(FC // 2):
            sg = wslot[0] % 4
            su = (wslot[0] + 1) % 4
            wslot[0] += 2
            dma("sp", wgu[:, sg, :, :], wg_d[:, :, fb * 256:(fb + 1) * 256], reads=[("wb", "wg" + sfx, l)], writes=[("wgu", sg)])
            dma("sp", wgu[:, su, :, :], wu_d[:, :, fb * 256:(fb + 1) * 256], reads=[("wb", "wu" + sfx, l)], writes=[("wgu", su)])
            for cc in range(2):
                f = fb * 2 + cc
                pb = f % 2

                def gmm(e, sg=sg, cc=cc, pb=pb):
                    for k in range(KC):
                        ii = e.matmul(ps[pb][:], lhsT=wgu[:, sg, k, cc * 128:(cc + 1) * 128], rhs=hT[:, k, :],
                                      start=(k == 0), stop=(k == KC - 1))
                    return ii

                def umm(e, su=su, cc=cc, pb=pb):
                    for k in range(KC):
                        ii = e.matmul(ps[2 + pb][:], lhsT=wgu[:, su, k, cc * 128:(cc + 1) * 128], rhs=hT[:, k, :],
                                      start=(k == 0), stop=(k == KC - 1))
                    return ii
                P.op("pe", gmm, reads=[("wgu", sg), "hT"], writes=[("ps", pb)])
                P.op("pe", umm, reads=[("wgu", su), "hT"], writes=[("ps", 2 + pb)])
                P.op("act", lambda e, pb=pb: e.activation(out=tmpA[:, pb, :], in_=ps[pb][:], func=AF.Silu),
                     reads=[("ps", pb)], writes=["tmpA%d" % pb])
                P.op("dve", lambda e, pb=pb, f=f: e.tensor_tensor(out=aT[:, f, :], in0=ps[2 + pb][:], in1=tmpA[:, pb, :], op=ALU.mult),
                     reads=[("ps", 2 + pb), "tmpA%d" % pb], writes=[("aTc", f)])
        aT_all = [("aTc", f) for f in range(FC)]
        for dp in range(KC // 2):
            slot = dslot[0] % 2
            dslot[0] += 1
            dma("sp", wd[:, slot, :, :], wd_d[:, :, dp * 256:(dp + 1) * 256], reads=[("wb", "wd" + sfx, l)], writes=[("wd", slot)])
            for cc in range(2):
                c = dp * 2 + cc
                pb = 4 + (c % 2)

                def dmm(e, slot=slot, cc=cc, pb=pb):
                    for f in range(FC):
                        ii = e.matmul(ps[pb][:], lhsT=wd[:, slot, f, cc * 128:(cc + 1) * 128], rhs=aT[:, f, :],
                                      start=(f == 0), stop=(f == FC - 1))
                    return ii
                P.op("pe", dmm, reads=[("wd", slot)] + aT_all, writes=[("ps", pb)])
                P.op("dve", lambda e, c=c, pb=pb: e.scalar_tensor_tensor(out=xT[:, c, :], in0=ps[pb][:], scalar=0.5,
                                                                         in1=xT[:, c, :], op0=ALU.mult, op1=ALU.add),
                     reads=[("ps", pb), "xT"], writes=["xT"])

    def rope_tables(tt):
        dma("sp", posb, pos_in[:, tt * T:(tt + 1) * T].partition_broadcast(128), writes=["posb"])
        P.op("dve", lambda e: e.tensor_copy(out=ang, in_=posb), reads=["posb"], writes=["ang"])
        P.op("dve", lambda e: e.tensor_scalar(out=ang, in0=ang, scalar1=cst[:, C_INVF:C_INVF + 1], scalar2=None, op0=ALU.mult),
             reads=["ang", "cst"], writes=["ang"])

        def reduce_sin(src, dst, key):
            P.op("dve", lambda e: e.tensor_scalar(out=angi, in0=src, scalar1=1.0 / TWO_PI, scalar2=None, op0=ALU.mult),
                 reads=[key], writes=["angi"])
            P.op("dve", lambda e: e.tensor_copy(out=t1, in_=angi), reads=["angi"], writes=["t1"])
            P.op("dve", lambda e: e.scalar_tensor_tensor(out=t2, in0=t1, scalar=-TWO_PI, in1=src, op0=ALU.mult, op1=ALU.add),
                 reads=["t1", key], writes=["t2"])
            P.op("dve", lambda e: e.tensor_scalar(out=t2, in0=t2, scalar1=-PI, scalar2=PI, op0=ALU.max, op1=ALU.min),
                 reads=["t2"], writes=["t2"])
            P.op("act", lambda e: e.activation(out=dst, in_=t2, func=AF.Sin), reads=["t2"], writes=[key + "_o"])
        reduce_sin(ang, sin2, "ang")
        P.op("dve", lambda e: e.tensor_scalar(out=sin2, in0=sin2, scalar1=cst[:, C_SGN:C_SGN + 1], scalar2=None, op0=ALU.mult),
             reads=["ang_o", "cst"], writes=["sin2"])
        P.op("dve", lambda e: e.tensor_scalar(out=ang2, in0=ang, scalar1=PI / 2, scalar2=None, op0=ALU.add),
             reads=["ang"], writes=["ang2"])
        reduce_sin(ang2, cos2, "ang2")
        P.op("dve", lambda e: e.tensor_copy(out=cos2, in_=cos2), reads=["ang2_o"], writes=["cos2"])

    def projections(tt, l):
        win_d = w_b["win", l].rearrange("(k p) c -> p k c", p=128)
        winsw_d = w_b["winsw", l].rearrange("(k p) c -> p k c", p=128)
        tsl = slice(tt * T, (tt + 1) * T)
        pbank = [0]
        stg = [0]

        def load_w(src, c0, n, key):
            s = wslot[0] % 4
            wslot[0] += 1
            dma("sp", wgu[:, s, :, 0:n], src[:, :, c0:c0 + n], reads=[key], writes=[("wgu", s)])
            return s

        def mm_fm(s, coff, ncols, bank, rhs=hT, rkey="hT", nk=KC):
            def fn(e):
                for k in range(nk):
                    ii = e.matmul(ps[bank][0:ncols, :], lhsT=wgu[:, s, k, coff:coff + ncols], rhs=rhs[:, k, :],
                                  start=(k == 0), stop=(k == nk - 1))
                return ii
            P.op("pe", fn, reads=[("wgu", s), rkey], writes=[("ps", bank)])

        def nextbank():
            b_ = pbank[0] % 6
            pbank[0] += 1
            return b_

        def nextstage():
            s_ = stg[0] % 2
            stg[0] += 1
            return s_

        def store_fm(bank, nrows, dst, scale=None, wkey=None, f32=False):
            if f32:
                P.op("act", lambda e: e.activation(out=stage32[0:nrows, :], in_=ps[bank][0:nrows, :], func=AF.Copy),
                     reads=[("ps", bank)], writes=["stage32"])
                dma("sp", dst, stage32[0:nrows, :], reads=["stage32"], writes=[wkey])
                return
            s_ = nextstage()
            if scale is None:
                P.op("act", lambda e: e.activation(out=stage[0:nrows, s_, :], in_=ps[bank][0:nrows, :], func=AF.Copy),
                     reads=[("ps", bank)], writes=[("stage", s_)])
            else:
                P.op("dve", lambda e: e.tensor_scalar(out=stage[0:nrows, s_, :], in0=ps[bank][0:nrows, :], scalar1=scale, scalar2=None, op0=ALU.mult),
                     reads=[("ps", bank)], writes=[("stage", s_)])
            dma("sp", dst, stage[0:nrows, s_, :], reads=[("stage", s_)], writes=[wkey])

        def rope_store(bank_a, bank_b, nrows, dst, scale, wkey):
            s_ = nextstage()
            P.op("dve", lambda e: e.scalar_tensor_tensor(out=t1[0:nrows, :], in0=ps[bank_a][0:nrows, :], scalar=scale,
                                                         in1=cos2[0:nrows, :], op0=ALU.mult, op1=ALU.mult),
                 reads=[("ps", bank_a), "cos2"], writes=["t1"])
            P.op("dve", lambda e: e.scalar_tensor_tensor(out=t2[0:nrows, :], in0=ps[bank_b][0:nrows, :], scalar=scale,
                                                         in1=sin2[0:nrows, :], op0=ALU.mult, op1=ALU.mult),
                 reads=[("ps", bank_b), "sin2"], writes=["t2"])
            P.op("dve", lambda e: e.tensor_tensor(out=stage[0:nrows, s_, :], in0=t1[0:nrows, :], in1=t2[0:nrows, :], op=ALU.add),
                 reads=["t1", "t2"], writes=[("stage", s_)])
            dma("sp", dst, stage[0:nrows, s_, :], reads=[("stage", s_)], writes=[wkey])

        kw = ("wb", "win", l)
        kws = ("wb", "winsw", l)
        for blk in range(2):
            s = load_w(win_d, O_CQ + blk * 256, 256, kw)
            for cc in range(2):
                b_ = nextbank()
                mm_fm(s, cc * 128, 128, b_)
                ch = blk * 2 + cc
                P.op("act", lambda e, b_=b_, ch=ch: e.activation(out=cq32[:, ch, :], in_=ps[b_][:], func=AF.Copy),
                     reads=[("ps", b_)], writes=[("cq32", ch)])
        s = load_w(win_d, O_CKV, 256, kw)
        for cc in range(2):
            b_ = nextbank()
            mm_fm(s, cc * 128, 128, b_)
            P.op("act", lambda e, b_=b_, cc=cc: e.activation(out=ckv32[:, cc, :], in_=ps[b_][:], func=AF.Copy),
                 reads=[("ps", b_)], writes=[("ckv32", cc)])
        s = load_w(win_d, O_KR, 64, kw)
        s2 = load_w(winsw_d, 0, 64, kws)
        ba, bb = nextbank(), nextbank()
        mm_fm(s, 0, 64, ba)
        mm_fm(s2, 0, 64, bb)
        rope_store(ba, bb, 64, g_in["Kpe", l][:, tsl], 1.0, ("gi", "Kpe", l, tt))
        for blk in range(2):
            s = load_w(win_d, O_QS + blk * 256, 256, kw)
            s2 = load_w(winsw_d, 64 + blk * 256, 256, kws)
            for cc in range(2):
                ba, bb = nextbank(), nextbank()
                mm_fm(s, cc * 128, 128, ba)
                mm_fm(s2, cc * 128, 128, bb)
                ch = blk * 2 + cc
                rope_store(ba, bb, 128, Qs_d[l][ch * 128:(ch + 1) * 128, tsl], 0.125, ("Qs", l, tt, ch))
        s = load_w(win_d, O_KS, 128, kw)
        s2 = load_w(winsw_d, 576, 128, kws)
        ba, bb = nextbank(), nextbank()
        mm_fm(s, 0, 128, ba)
        mm_fm(s2, 0, 128, bb)
        rope_store(ba, bb, 128, g_in["Ks", l][:, tsl], 1.0, ("gi", "Ks", l, tt))
        for blk in range(2):
            s = load_w(win_d, O_QF + blk * 256, 256, kw)
            for cc in range(2):
                b_ = nextbank()
                mm_fm(s, cc * 128, 128, b_)
                ch = blk * 2 + cc
                store_fm(b_, 128, Qf_d[l][2 * ch:2 * ch + 2, 0:64, tsl], scale=0.125, wkey=("Qf", l, tt, ch))
        for blk in range(2):
            s = load_w(win_d, O_KF + blk * 256, 256, kw)
            for cc in range(2):
                b_ = nextbank()
                mm_fm(s, cc * 128, 128, b_)
                ch = blk * 2 + cc
                store_fm(b_, 128, g_in["Kf", l][ch * 128:(ch + 1) * 128, tsl], wkey=("gi", "Kf", l, tt, ch))
        s = load_w(win_d, O_FL, 8, kw)
        b_ = nextbank()
        mm_fm(s, 0, 8, b_)
        store_fm(b_, 8, g_in["FL", l][:, tsl], wkey=("gi", "FL", l, tt), f32=True)
        sv = load_w(win_d, O_VS, 128, kw)
        for tb in range(4):
            b_ = nextbank()

            def fn(e, tb=tb, b_=b_):
                for k in range(KC):
                    ii = e.matmul(ps[b_][:, 0:128], lhsT=hT[:, k, tb * 128:(tb + 1) * 128], rhs=wgu[:, sv, k, 0:128],
                                  start=(k == 0), stop=(k == KC - 1))
                return ii
            P.op("pe", fn, reads=[("wgu", sv), "hT"], writes=[("ps", b_)])
            s_ = nextstage()
            P.op("act", lambda e, b_=b_, s_=s_: e.activation(out=stage[:, s_, 0:128], in_=ps[b_][:, 0:128], func=AF.Copy),
                 reads=[("ps", b_)], writes=[("stage", s_)])
            dma("sp", g_in["Vs", l][tt * T + tb * 128: tt * T + (tb + 1) * 128, :], stage[:, s_, 0:128],
                reads=[("stage", s_)], writes=[("gi", "Vs", l, tt, tb)])
        sva = load_w(win_d, O_VF, 256, kw)
        svb = load_w(win_d, O_VF + 256, 256, kw)
        for tb in range(4):
            b_ = nextbank()

            def fn(e, tb=tb, b_=b_):
                for half, sx in ((0, sva), (1, svb)):
                    for k in range(KC):
                        ii = e.matmul(ps[b_][:, half * 256:(half + 1) * 256], lhsT=hT[:, k, tb * 128:(tb + 1) * 128],
                                      rhs=wgu[:, sx, k, 0:256], start=(k == 0), stop=(k == KC - 1))
                return ii
            P.op("pe", fn, reads=[("wgu", sva), ("wgu", svb), "hT"], writes=[("ps", b_)])
            s_ = nextstage()
            P.op("act", lambda e, b_=b_, s_=s_: e.activation(out=stage[:, s_, :], in_=ps[b_][:], func=AF.Copy),
                 reads=[("ps", b_)], writes=[("stage", s_)])
            dma("sp", g_in["Vf", l][tt * T + tb * 128: tt * T + (tb + 1) * 128, :], stage[:, s_, :],
                reads=[("stage", s_)], writes=[("gi", "Vf", l, tt, tb)])

        def small_norm(src32, nch, srckey, dstn, dstkey, gcol, nfeat):
            P.op("act", lambda e: e.activation(out=sqs[:, 0:nch, :], in_=src32[:, 0:nch, :], func=AF.Square),
                 reads=[(srckey, c_) for c_ in range(nch)], writes=["sqs"])

            def ssmm(e):
                for k in range(nch):
                    i = e.matmul(ps[6][:], lhsT=ones[:], rhs=sqs[:, k, :], start=(k == 0), stop=(k == nch - 1))
                return i
            P.op("pe", ssmm, reads=["sqs", "ones"], writes=[("ps", 6)])
            rstd_from_ss(6, 1.0 / nfeat)

            def mkn(e):
                for k in range(nch):
                    i = e.scalar_tensor_tensor(out=dstn[:, k, :], in0=src32[:, k, :], scalar=cst[:, gcol + k:gcol + k + 1],
                                               in1=rstd[:], op0=ALU.mult, op1=ALU.mult)
                return i
            P.op("dve", mkn, reads=[(srckey, c_) for c_ in range(nch)] + ["rstd", "cst"], writes=[dstkey])
        small_norm(cq32, 4, "cq32", cqn, "cqn", C_QN[l], 512)
        QSC = float(192 ** -0.5)
        for h in range(8):
            b_ = nextbank()

            def fn(e, h=h, b_=b_):
                for k in range(4):
                    ii = e.matmul(ps[b_][:], lhsT=wqb_s[:, k, h * 128:(h + 1) * 128], rhs=cqn[:, k, :], start=(k == 0), stop=(k == 3))
                return ii
            P.op("pe", fn, reads=[("wqb_s", l), "cqn"], writes=[("ps", b_)])
            store_fm(b_, 128, QnT_d[l][h * 128:(h + 1) * 128, tsl], scale=QSC, wkey=("QnT", l, tt, h))
        for hp in range(4):
            ba, bb = nextbank(), nextbank()

            def fa(e, hp=hp, ba=ba):
                for k in range(4):
                    ii = e.matmul(ps[ba][:], lhsT=wqb_s[:, k, 1024 + hp * 128:1024 + (hp + 1) * 128], rhs=cqn[:, k, :], start=(k == 0), stop=(k == 3))
                return ii

            def fb_(e, hp=hp, bb=bb):
                for k in range(4):
                    ii = e.matmul(ps[bb][:], lhsT=wqbsw_s[:, k, hp * 128:(hp + 1) * 128], rhs=cqn[:, k, :], start=(k == 0), stop=(k == 3))
                return ii
            P.op("pe", fa, reads=[("wqb_s", l), "cqn"], writes=[("ps", ba)])
            P.op("pe", fb_, reads=[("wqb_s", l), "cqn"], writes=[("ps", bb)])
            rope_store(ba, bb, 128, QpT_d[l][hp * 128:(hp + 1) * 128, tsl], QSC, ("QpT", l, tt, hp))
        small_norm(ckv32, 2, "ckv32", ckvn, "ckvn", C_KVN[l], 256)
        for h in range(8):
            b_ = nextbank()

            def fn(e, h=h, b_=b_):
                for k in range(2):
                    ii = e.matmul(ps[b_][:], lhsT=wkvb_s[:, k, h * 128:(h + 1) * 128], rhs=ckvn[:, k, :], start=(k == 0), stop=(k == 1))
                return ii
            P.op("pe", fn, reads=[("wkvb_s", l), "ckvn"], writes=[("ps", b_)])
            store_fm(b_, 128, g_in["Kn", l][h * 128:(h + 1) * 128, tsl], wkey=("gi", "Kn", l, tt, h))
        for tb in range(4):
            for half in range(2):
                b_ = nextbank()

                def fn(e, tb=tb, half=half, b_=b_):
                    for k in range(2):
                        ii = e.matmul(ps[b_][:], lhsT=ckvn[:, k, tb * 128:(tb + 1) * 128],
                                      rhs=wkvb_s[:, k, 1024 + half * 512:1024 + (half + 1) * 512], start=(k == 0), stop=(k == 1))
                    return ii
                P.op("pe", fn, reads=[("wkvb_s", l), "ckvn"], writes=[("ps", b_)])
                s_ = nextstage()
                P.op("act", lambda e, b_=b_, s_=s_: e.activation(out=stage[:, s_, :], in_=ps[b_][:], func=AF.Copy),
                     reads=[("ps", b_)], writes=[("stage", s_)])
                dma("sp", g_in["Vm", l][tt * T + tb * 128: tt * T + (tb + 1) * 128, half * 512:(half + 1) * 512], stage[:, s_, :],
                    reads=[("stage", s_)], writes=[("gi", "Vm", l, tt, tb, half)])

    def load_layer_small_weights(l):
        dma("sp", wqb_s, w_b["wqb", l].rearrange("(k p) c -> p k c", p=128), reads=[("wb", "wqb", l)], writes=[("wqb_s", l)])
        dma("sp", wqbsw_s, w_b["wqbsw", l].rearrange("(k p) c -> p k c", p=128), reads=[("wb", "wqbsw", l)], writes=[("wqb_s", l)])
        dma("sp", wkvb_s, w_b["wkvb", l].rearrange("(k p) c -> p k c", p=128), reads=[("wb", "wkvb", l)], writes=[("wkvb_s", l)])

    def phase_a_tile(tt, l):
        rmsnorm_to_hT(C_FFN1N[l])
        ffn(0, l)
        rmsnorm_to_hT(C_MIXN[l])
        xres_v = xres.rearrange("(k p) n -> p k n", p=128)
        dma("sp", xres_v[:, :, tt * T:(tt + 1) * T], xT[:], reads=["xT"], writes=[("xres", tt)])
        P.barrier()
        load_layer_small_weights(l)
        rope_tables(tt)
        projections(tt, l)
        P.barrier()

    def gather_all(l):
        for nm, R, C, dt in GSPEC:
            allgather(g_in[nm, l], g_out[nm, l], reads=[], writes=[("go", nm, l)])

    def fox_c_prep(l, b):
        FLg = g_out["FL", l].rearrange("(r h) n -> r h n", r=NR)
        for rr in range(NR):
            dma("sp", FLs4[0:8, :, rr, :], FLg[rr, :, b * 1024:(b + 1) * 1024].rearrange("h (m q) -> h m q", m=8),
                writes=[("FLs", rr)])
        fl_all = [("FLs", rr) for rr in range(NR)]
        P.op("dve", lambda e: e.tensor_scalar(out=small[0:8, 0:1], in0=cst[0:8, C_FB[l]:C_FB[l] + 1], scalar1=-1.0, scalar2=None, op0=ALU.mult),
             reads=["cst"], writes=["nfb"])
        P.op("act", lambda e: e.activation(out=FLs[0:8, :], in_=FLs[0:8, :], func=AF.Exp, bias=small[0:8, 0:1], scale=-1.0),
             reads=fl_all + ["nfb"], writes=["FLs"])
        P.op("act", lambda e: e.activation(out=FLs[0:8, :], in_=FLs[0:8, :], func=AF.Ln, bias=1.0),
             reads=["FLs"], writes=["FLs"])
        P.op("dve", lambda e: e.tensor_tensor_scan(out=Cs[0:8, :], data0=FLs[0:8, :], data1=FLs[0:8, :], initial=0.0,
                                                  op0=ALU.add, op1=ALU.max),
             reads=["FLs"], writes=["Cs"])
        def tr(e):
            for m in range(8):
                for rr in range(8):
                    g = m * 8 + rr
                    ii = e.transpose(out=ps[7][:, (rr * 8 + m) * 8:(rr * 8 + m) * 8 + 8], in_=Cs[0:8, g * 128:(g + 1) * 128],
                                     identity=cst[0:8, C_ID8:C_ID8 + 8])
            return ii
        P.op("pe", tr, reads=["Cs", "cst"], writes=[("ps", 7)])
        P.op("dve", lambda e: e.tensor_copy(out=cpT[:, b, :, :, :].rearrange("p r m h -> p (r m h)"), in_=ps[7][:]),
             reads=[("ps", 7)], writes=[("cpT", b)])
        cown3 = cown[0:8, :].rearrange("p (m q) -> p m q", m=8)
        P.op("dve", lambda e: e.tensor_scalar(out=cown3, in0=Cs4[0:8, :, 0, :], scalar1=cst[0:8, C_OH:C_OH + 1], scalar2=None, op0=ALU.mult),
             reads=["Cs", "cst"], writes=["cown"])
        for rr in range(1, 8):
            P.op("dve", lambda e, rr=rr: e.scalar_tensor_tensor(out=cown3, in0=Cs4[0:8, :, rr, :], scalar=cst[0:8, C_OH + rr:C_OH + rr + 1],
                                                                 in1=cown3, op0=ALU.mult, op1=ALU.add),
                 reads=["Cs", "cown"], writes=["cown"])
        P.op("dve", lambda e: e.tensor_scalar(out=cneg[0:8, :], in0=cown[0:8, :], scalar1=-1.0, scalar2=None, op0=ALU.mult),
             reads=["cown"], writes=["cneg"])
        P.op("dve", lambda e: e.tensor_copy(out=cq3v[0:8, 0, :], in_=cneg[0:8, :]), reads=["cneg"], writes=["cq3a"])
        P.op("dve", lambda e: e.tensor_tensor(out=r1[0:8, :], in0=cneg[0:8, :], in1=cq3v[0:8, 0, :], op=ALU.subtract),
             reads=["cneg", "cq3a"], writes=["r1"])
        P.op("dve", lambda e: e.tensor_copy(out=cq3v[0:8, 1, :], in_=r1[0:8, :]), reads=["r1"], writes=["cq3b"])
        P.op("dve", lambda e: e.tensor_tensor(out=cneg[0:8, :], in0=r1[0:8, :], in1=cq3v[0:8, 1, :], op=ALU.subtract),
             reads=["r1", "cq3b"], writes=["cneg"])
        P.op("dve", lambda e: e.tensor_copy(out=cq3v[0:8, 2, :], in_=cneg[0:8, :]), reads=["cneg"], writes=["cq3c"])
        dma("sp", Qf_d[l][:, 64:67, b * 1024:(b + 1) * 1024], cq3v[0:8, :, :], reads=["cq3a", "cq3b", "cq3c"],
            writes=[("Qfc", l, b)])

    def swa_select(l, b):
        Ksg = g_out["Ks", l].rearrange("(r g d) n -> r d g n", r=NR, g=2)
        Vsg = g_out["Vs", l].rearrange("(r n) d -> r n d", r=NR)
        bs = slice(b * 1024, (b + 1) * 1024)
        dma("sp", KsOwn[0:64, :, :], g_in["Ks", l].rearrange("(g d) n -> d g n", g=2)[:, :, bs], reads=[], writes=["KsOwn"])
        dma("sp", VsOwn[:, :, :], g_in["Vs", l][bs, :].rearrange("(m p) d -> p m d", p=128), reads=[], writes=["VsOwn"])
        for g in range(2):
            for rr in range(NR):
                dma("sp", selG[0:64, rr, :], Ksg[rr, :, g, bs], reads=[("go", "Ks", l)], writes=[("selG", rr)])
            allg = [("selG", rr) for rr in range(NR)]
            P.op("dve", lambda e: e.tensor_scalar(out=selacc[0:64, :], in0=selG[0:64, 0, :], scalar1=cst[0:64, C_SELA:C_SELA + 1],
                                                  scalar2=None, op0=ALU.mult), reads=allg + ["cst"], writes=["selacc"])
            for rr in range(1, NR):
                P.op("dve", lambda e, rr=rr: e.scalar_tensor_tensor(out=selacc[0:64, :], in0=selG[0:64, rr, :],
                                                                     scalar=cst[0:64, C_SELA + rr:C_SELA + rr + 1], in1=selacc[0:64, :],
                                                                     op0=ALU.mult, op1=ALU.add), reads=["selacc", ("selG", rr)], writes=["selacc"])
            for rr in range(NR):
                P.op("dve", lambda e, rr=rr: e.scalar_tensor_tensor(out=selacc[0:64, 128:1024], in0=selG[0:64, rr, 0:896],
                                                                     scalar=cst[0:64, C_SELB + rr:C_SELB + rr + 1], in1=selacc[0:64, 128:1024],
                                                                     op0=ALU.mult, op1=ALU.add), reads=["selacc", ("selG", rr)], writes=["selacc"])
            P.op("dve", lambda e, g=g: e.tensor_copy(out=KsPrev[0:64, g, :], in_=selacc[0:64, :]), reads=["selacc"], writes=[("KsPrev", g)])
        selV = selG.rearrange("p r (m d) -> p r m d", m=8)
        accV = selacc.rearrange("p (m d) -> p m d", m=8)
        for rr in range(NR):
            dma("sp", selV[:, rr, :, :], Vsg[rr, bs, :].rearrange("(m p) d -> p m d", p=128), reads=[("go", "Vs", l)] + [("KsPrev", 1)],
                writes=[("selG", rr)])
        allg = [("selG", rr) for rr in range(NR)]
        P.op("dve", lambda e: e.tensor_scalar(out=selacc[:, :], in0=selG[:, 0, :], scalar1=cst[:, C_SELA:C_SELA + 1],
                                              scalar2=None, op0=ALU.mult), reads=allg + ["cst"], writes=["selacc"])
        for rr in range(1, NR):
            P.op("dve", lambda e, rr=rr: e.scalar_tensor_tensor(out=selacc[:, :], in0=selG[:, rr, :],
                                                                 scalar=cst[:, C_SELA + rr:C_SELA + rr + 1], in1=selacc[:, :],
                                                                 op0=ALU.mult, op1=ALU.add), reads=["selacc", ("selG", rr)], writes=["selacc"])
        for rr in range(NR):
            P.op("dve", lambda e, rr=rr: e.scalar_tensor_tensor(out=accV[:, 1:8, :], in0=selV[:, rr, 0:7, :],
                                                                 scalar=cst[:, C_SELB + rr:C_SELB + rr + 1], in1=accV[:, 1:8, :],
                                                                 op0=ALU.mult, op1=ALU.add), reads=["selacc", ("selG", rr)], writes=["selacc"])
        P.op("dve", lambda e: e.tensor_copy(out=VsPrev[:, :, :], in_=accV), reads=["selacc"], writes=["VsPrev"])

    rot = {"k": 0, "q": 0, "s": 0, "p": 0, "o": 0}

    def nxt(key, n):
        v = rot[key] % n
        rot[key] += 1
        return v

    def attn_swa(l, tt):
        b, i = tt // 2, tt % 2
        dma("sp", qs_sb[0:64, :, :], Qs_d[l].rearrange("(h d) n -> d h n", d=64)[:, :, tt * T:(tt + 1) * T],
            reads=[("Qs", l, tt, ch) for ch in range(4)], writes=["qs_sb"])
        P.op("act", lambda e: e.activation(out=small[:, 8:12], in_=cst[:, C_SINK[l]:C_SINK[l] + 4], func=AF.Exp),
             reads=["cst"], writes=["esink"])
        for a in range(4):
            m = 4 * i + a
            for g in range(2):
                ob = 3 + nxt("o", 2)
                db = ob + 2
                plist = []
                for which in range(2):
                    Ksrc = KsPrev if which == 0 else KsOwn
                    kkey = ("KsPrev", g) if which == 0 else "KsOwn"
                    sbk = nxt("s", 3)
                    pbf = nxt("p", 4)
                    P.op("pe", lambda e, Ksrc=Ksrc, sbk=sbk, g=g, m=m, a=a: e.matmul(
                        ps[sbk][:], lhsT=Ksrc[0:64, g, m * 128:(m + 1) * 128], rhs=qs_sb[0:64, 4 * g:4 * g + 4, a * 128:(a + 1) * 128],
                        start=True, stop=True), reads=[kkey, "qs_sb"], writes=[("ps", sbk)])
                    P.op("act", lambda e, sbk=sbk, pbf=pbf: e.activation(out=Pb[:, pbf, :], in_=ps[sbk][:], func=AF.Exp),
                         reads=[("ps", sbk)], writes=[("P", pbf)])
                    if which == 0:
                        mi = M_PREV0 if m == 0 else M_SUP
                    else:
                        mi = M_TRIL

                    def mfn(e, pbf=pbf, mi=mi):
                        for j in range(4):
                            ii = e.tensor_tensor(out=Pb[:, pbf, j * 128:(j + 1) * 128], in0=Pb[:, pbf, j * 128:(j + 1) * 128],
                                                 in1=mk[:, mi, :], op=ALU.mult)
                        return ii
                    P.op("dve", mfn, reads=[("P", pbf), "mk"], writes=[("P", pbf)])
                    plist.append((which, pbf))

                def pv(e, plist=plist, g=g, m=m, ob=ob, db=db):
                    for j in range(4):
                        h = 4 * g + j
                        ro = (h % 2) * 64
                        co = (j // 2) * 128
                        for n_, (which, pbf) in enumerate(plist):
                            Vsrc = VsPrev if which == 0 else VsOwn
                            e.matmul(ps[ob][ro:ro + 64, co:co + 128], lhsT=Vsrc[:, m, g * 64:(g + 1) * 64],
                                     rhs=Pb[:, pbf, j * 128:(j + 1) * 128], start=(n_ == 0), stop=(n_ == 1))
                            ii = e.matmul(ps[db][ro:ro + 64, co:co + 128], lhsT=ones[:, 0:64],
                                          rhs=Pb[:, pbf, j * 128:(j + 1) * 128], start=(n_ == 0), stop=(n_ == 1))
                    return ii
                P.op("pe", pv, reads=[("P", p_) for _, p_ in plist] + ["VsPrev", "VsOwn", "ones"], writes=[("ps", ob), ("ps", db)])
                for pr in range(2):
                    P.op("dve", lambda e, pr=pr, g=g, db=db: e.tensor_scalar(out=tmpA[:, 0, pr * 128:(pr + 1) * 128], in0=ps[db][:, pr * 128:(pr + 1) * 128],
                                                                             scalar1=small[:, 8 + 2 * g + pr:9 + 2 * g + pr], scalar2=None, op0=ALU.add),
                         reads=[("ps", db), "esink"], writes=["tmpA0"])
                P.op("dve", lambda e: e.reciprocal(out=tmpA[:, 1, 0:256], in_=tmpA[:, 0, 0:256]), reads=["tmpA0"], writes=["tmpA1"])
                for pr in range(2):
                    P.op("dve", lambda e, pr=pr, g=g, ob=ob, a=a: e.tensor_tensor(out=hT[:, 8 + 2 * g + pr, a * 128:(a + 1) * 128],
                                                                                  in0=ps[ob][:, pr * 128:(pr + 1) * 128],
                                                                                  in1=tmpA[:, 1, pr * 128:(pr + 1) * 128], op=ALU.mult),
                         reads=[("ps", ob), "tmpA1"], writes=[("mixed", 8 + 2 * g + pr)])

    def attn_dense(l, tt, kind, h):
        b, i = tt // 2, tt % 2
        nm = 4 * i + 4
        ntok = nm * 128
        tsl = slice(tt * T, (tt + 1) * T)
        qs = nxt("q", 2)
        if kind == "mla":
            dma("sp", Qn[:, qs, :], QnT_d[l][h * 128:(h + 1) * 128, tsl], reads=[("QnT", l, tt, h)], writes=[("Qn", qs)])
            dma("sp", Qp[0:64, qs, :], QpT_d[l][h * 64:(h + 1) * 64, tsl], reads=[("QpT", l, tt, h // 2)], writes=[("Qp", qs)])
            Kg = g_out["Kn", l].rearrange("(r a) n -> r a n", r=NR)
            Vg = g_out["Vm", l].rearrange("(r n) d -> r n d", r=NR)
            dv, ro, chunk = 128, 0, h
        else:
            dma("sp", Qn[0:67, qs, :], Qf_d[l][h, :, tsl], reads=[("Qf", l, tt, h // 2), ("Qfc", l, b)], writes=[("Qn", qs)])
            Kg = g_out["Kf", l].rearrange("(r a) n -> r a n", r=NR)
            Vg = g_out["Vf", l].rearrange("(r n) d -> r n d", r=NR)
            dv, ro, chunk = 64, (h % 2) * 64, 12 + h // 2
        ob = 3 + nxt("o", 2)
        db = ob + 2
        ks_of = {}

        def load_rank(rr):
            ks = nxt("k", 3)
            if kind == "mla":
                dma("sp", Kc[:, ks, 0:ntok], Kg[rr, h * 128:(h + 1) * 128, b * 1024:b * 1024 + ntok],
                    reads=[("go", "Kn", l)], writes=[("Kc", ks)])
                dma("sp", Vc[:, ks, 0:nm, :], Vg[rr, b * 1024:b * 1024 + ntok, h * 128:(h + 1) * 128].rearrange("(m p) d -> p m d", p=128),
                    reads=[("go", "Vm", l)], writes=[("Vc", ks)])
            else:
                dma("sp", Kc[0:64, ks, 0:ntok], Kg[rr, h * 64:(h + 1) * 64, b * 1024:b * 1024 + ntok],
                    reads=[("go", "Kf", l), "Kc_ones"], writes=[("Kc", ks)])
                dma("sp", Vc[:, ks, 0:nm, 0:64], Vg[rr, b * 1024:b * 1024 + ntok, h * 64:(h + 1) * 64].rearrange("(m p) d -> p m d", p=128),
                    reads=[("go", "Vf", l)], writes=[("Vc", ks)])
            ks_of[rr] = ks
        for rr in range(3):
            load_rank(rr)
        blocks = [(rr, mp) for rr in range(NR) for mp in range(nm)]
        nblk = len(blocks)
        pend = []

        def emit_s(idx):
            rr, mp = blocks[idx]
            ks = ks_of[rr]
            a = mp - 4 * i
            c0 = max(a, 0) * 128
            sbk = nxt("s", 3)
            pbf = nxt("p", 4)
            msk = a >= 0
            if kind == "mla":
                def fn(e):
                    e.matmul(ps[sbk][:, c0:], lhsT=Kc[:, ks, mp * 128:(mp + 1) * 128], rhs=Qn[:, qs, c0:], start=True, stop=False)
                    ii = e.matmul(ps[sbk][:, c0:], lhsT=KpeAll[0:64, rr, mp * 128:(mp + 1) * 128], rhs=Qp[0:64, qs, c0:], start=False, stop=not msk)
                    if msk:
                        ii = e.matmul(ps[sbk][:, c0:c0 + 128], lhsT=mk[:, M_ID, :], rhs=mk[:, rr, :], start=False, stop=True)
                    return ii
                P.op("pe", fn, reads=[("Kc", ks), ("Qn", qs), ("Qp", qs), ("KpeAll", rr), "mk"], writes=[("ps", sbk)])
                P.op("act", lambda e: e.activation(out=Pb[:, pbf, c0:], in_=ps[sbk][:, c0:], func=AF.Exp),
                     reads=[("ps", sbk)], writes=[("P", pbf)])
            else:
                def fn(e):
                    ii = e.matmul(ps[sbk][:, c0:], lhsT=Kc[0:67, ks, mp * 128:(mp + 1) * 128], rhs=Qn[0:67, qs, c0:], start=True, stop=not msk)
                    if msk:
                        ii = e.matmul(ps[sbk][:, c0:c0 + 128], lhsT=mk[:, M_ID, :], rhs=mk[:, rr, :], start=False, stop=True)
                    return ii
                P.op("pe", fn, reads=[("Kc", ks), ("Qn", qs), "mk"], writes=[("ps", sbk)])
                P.op("act", lambda e: e.activation(out=Pb[:, pbf, c0:], in_=ps[sbk][:, c0:], func=AF.Exp,
                                                   bias=cpT[:, b, rr, mp, h:h + 1]),
                     reads=[("ps", sbk), ("cpT", b)], writes=[("P", pbf)])
            return (idx, pbf, c0)

        def emit_pv(idx, pbf, c0):
            rr, mp = blocks[idx]
            ks = ks_of[rr]
            first, last = idx == 0, idx == nblk - 1

            def fn(e):
                e.matmul(ps[ob][ro:ro + dv, c0:], lhsT=Vc[:, ks, mp, 0:dv], rhs=Pb[:, pbf, c0:], start=first, stop=last)
                return e.matmul(ps[db][ro:ro + dv, c0:], lhsT=ones[:, 0:dv], rhs=Pb[:, pbf, c0:], start=first, stop=last)
            P.op("pe", fn, reads=[("P", pbf), ("Vc", ks), "ones"], writes=[("ps", ob), ("ps", db)])
            if mp == nm - 1 and rr + 3 < NR:
                load_rank(rr + 3)

        SKEW = 2
        for idx in range(nblk):
            pend.append(emit_s(idx))
            if len(pend) > SKEW:
                emit_pv(*pend.pop(0))
        while pend:
            emit_pv(*pend.pop(0))
        P.op("dve", lambda e: e.reciprocal(out=rstd[ro:ro + dv, :], in_=ps[db][ro:ro + dv, :]), reads=[("ps", db)], writes=["rstd"])
        P.op("dve", lambda e: e.tensor_tensor(out=hT[ro:ro + dv, chunk, :], in0=ps[ob][ro:ro + dv, :], in1=rstd[ro:ro + dv, :], op=ALU.mult),
             reads=[("ps", ob), "rstd"], writes=[("mixed", chunk, ro)])

    def attention_tile(l, tt):
        b, i = tt // 2, tt % 2
        nm = 4 * i + 4
        ntok = nm * 128
        if i == 0:
            P.barrier()
            fox_c_prep(l, b)
            P.barrier()
            swa_select(l, b)
            P.barrier()
        attn_swa(l, tt)
        Kpg = g_out["Kpe", l].rearrange("(r a) n -> r a n", r=NR)
        for rr in range(NR):
            dma("sp", KpeAll[0:64, rr, 0:ntok], Kpg[rr, :, b * 1024:b * 1024 + ntok], reads=[("go", "Kpe", l)], writes=[("KpeAll", rr)])
        for h in range(8):
            attn_dense(l, tt, "mla", h)
        P.op("dve", lambda e: e.memset(Kc[64:67, :, :], 1.0), reads=[("Kc", 0), ("Kc", 1), ("Kc", 2)], writes=["Kc_ones", ("Kc", 0), ("Kc", 1), ("Kc", 2)])
        for h in range(8):
            attn_dense(l, tt, "fox", h)

    def wout_and_residual(l, tt):
        xres_v = xres.rearrange("(k p) n -> p k n", p=128)
        dma("sp", xT[:], xres_v[:, :, tt * T:(tt + 1) * T], reads=[("xres", tt)], writes=["xT"])
        wo_d = w_b["wout", l].rearrange("(k p) c -> p k c", p=128)
        mixed_all = [("mixed", c_) for c_ in range(8, 12)] + [("mixed", c_, 0) for c_ in range(8)] + \
                    [("mixed", c_, r_) for c_ in range(12, 16) for r_ in (0, 64)]
        if debug and l == 0:
            dma("sp", dbg_mixed.rearrange("(k p) n -> p k n", p=128)[:, :, tt * T:(tt + 1) * T], hT[:], reads=mixed_all, writes=[("dbgm", tt)])
        for blk in range(8):
            s = wslot[0] % 4
            wslot[0] += 1
            dma("sp", wgu[:, s, :, :], wo_d[:, :, blk * 256:(blk + 1) * 256], reads=[("wb", "wout", l)], writes=[("wgu", s)])
            for cc in range(2):
                c = blk * 2 + cc
                pb = c % 2

                def fn(e, s=s, cc=cc, pb=pb):
                    for k in range(KC):
                        ii = e.matmul(ps[pb][:], lhsT=wgu[:, s, k, cc * 128:(cc + 1) * 128], rhs=hT[:, k, :], start=(k == 0), stop=(k == KC - 1))
                    return ii
                P.op("pe", fn, reads=[("wgu", s)] + mixed_all, writes=[("ps", pb)])
                P.op("dve", lambda e, c=c, pb=pb: e.tensor_tensor(out=xT[:, c, :], in0=ps[pb][:], in1=xT[:, c, :], op=ALU.add),
                     reads=[("ps", pb), "xT"], writes=["xT"])
        if debug and l == 0:
            dma("sp", dbg_x.rearrange("(k p) n -> p k n", p=128)[:, :, tt * T:(tt + 1) * T], xT[:], reads=["xT"], writes=[("dbgx", tt)])

    L0 = ["wg1", "wu1", "wd1", "win", "winsw", "wqb", "wqbsw", "wkvb", "wout", "wg2", "wu2", "wd2"]
    for nm in L0:
        conv_weight(nm, 0)
    xin_v = xT_in.rearrange("(k p) n -> p k n", p=128)
    out_v = outT.rearrange("(k p) n -> p k n", p=128)
    for tt in range(NT):
        dma("sp", xT[:], xin_v[:, :, tt * T:(tt + 1) * T], writes=["xT"])
        phase_a_tile(tt, 0)
    out_ops = []
    for l in range(nlayers):
        P.barrier()
        gather_all(l)
        if l + 1 < nlayers:
            for nm in L0:
                conv_weight(nm, l + 1)
        for tt in range(NT):
            P.barrier()
            attention_tile(l, tt)
            P.barrier()
            wout_and_residual(l, tt)
            P.barrier()
            rmsnorm_to_hT(C_FFN2N[l])
            ffn(1, l)
            if l + 1 < nlayers:
                P.barrier()
                phase_a_tile(tt, l + 1)
            else:
                if nlayers == DEPTH:
                    rmsnorm_to_hT(C_FINN, out_f32=True)
                out_ops.append(dma("sp", out_v[:, :, tt * T:(tt + 1) * T], xT[:], reads=["xT"], writes=[("out", tt)]))
    P.barrier()
    P.emit(st)
    st.close()
    return nc


def _fm(v):
    return np.ascontiguousarray(v.reshape(-1, 128).T)


def make_consts(inp, r):
    c = np.zeros((128, NCONST), np.float32)
    for l in range(DEPTH):
        c[:, C_FFN1N[l]:C_FFN1N[l] + KC] = _fm(inp["ffn1_norm"][l])
        c[:, C_MIXN[l]:C_MIXN[l] + KC] = _fm(inp["mix_norm"][l])
        c[:, C_FFN2N[l]:C_FFN2N[l] + KC] = _fm(inp["ffn2_norm"][l])
        c[:, C_QN[l]:C_QN[l] + 4] = _fm(inp["mla_q_norm"][l])
        c[:, C_KVN[l]:C_KVN[l] + 2] = _fm(inp["mla_kv_norm"][l])
        sk = inp["swa_sinks"][l]
        for pr in range(4):
            c[0:64, C_SINK[l] + pr] = sk[2 * pr]
            c[64:128, C_SINK[l] + pr] = sk[2 * pr + 1]
        c[0:8, C_FB[l]] = inp["fox_forget_bias"][l]
    c[:, C_FINN:C_FINN + KC] = _fm(inp["final_norm"])
    inv = (np.float32(10000.0) ** (-(np.arange(0, 64, 2, dtype=np.float32)) / np.float32(64))).astype(np.float32)
    p = np.arange(128)
    c[:, C_INVF] = inv[p % 32]
    c[:, C_SGN] = np.where((p % 64) < 32, -1.0, 1.0)
    for rr in range(8):
        c[:, C_OH + rr] = 1.0 if rr == r else 0.0
        c[:, C_SELA + rr] = 1.0 if (r >= 1 and rr == r - 1) else 0.0
        c[:, C_SELB + rr] = 1.0 if (r == 0 and rr == 7) else 0.0
    c[0:8, C_ID8:C_ID8 + 8] = np.eye(8, dtype=np.float32)
    return c


def make_masks(r):
    pk = np.arange(128)[:, None]
    pq = np.arange(128)[None, :]
    tril = (pk <= pq).astype(np.float32)
    sup = (pk > pq).astype(np.float32)
    m = np.zeros((128, NMASK, 128), np.float32)
    for rr in range(8):
        if rr < r:
            m[:, rr] = 0.0
        elif rr == r:
            m[:, rr] = NEG * sup
        else:
            m[:, rr] = NEG
    m[:, M_ID] = np.eye(128, dtype=np.float32)
    m[:, M_TRIL] = tril
    m[:, M_SUP] = sup
    m[:, M_PREV0] = 0.0 if r == 0 else sup
    return np.ascontiguousarray(m.reshape(128, NMASK * 128)).astype(ml_dtypes.bfloat16)


def shard_tokens(a, r):
    sh = a.shape
    v = a.reshape((2, 8, 8, 128) + sh[2:])[:, :, r]
    return v.reshape((NTOK,) + sh[2:])


def unshard_out(outs, dtype):
    full = np.zeros((2, S, D), dtype)
    fv = full.reshape(2, 8, 8, 128, D)
    for r, o in enumerate(outs):
        fv[:, :, r] = o.T.reshape(2, 8, 128, D)
    return full


def _swap_halves(w, starts):
    cols = []
    for s0 in starts:
        cols.extend(range(s0 + 32, s0 + 64))
        cols.extend(range(s0, s0 + 32))
    return w[..., cols]


def prep_weights(inp):
    W = {}
    W["wg1"], W["wu1"], W["wd1"] = inp["ffn1_w_gate"], inp["ffn1_w_up"], inp["ffn1_w_down"]
    W["wg2"], W["wu2"], W["wd2"] = inp["ffn2_w_gate"], inp["ffn2_w_up"], inp["ffn2_w_down"]
    W["win"] = inp["w_in"]
    W["winsw"] = _swap_halves(inp["w_in"], [O_KR] + [O_QS + 64 * h for h in range(8)] + [O_KS, O_KS + 64])
    qb = inp["mla_w_q_b"]
    nope = [h * 192 + j for h in range(8) for j in range(128)]
    pe = [h * 192 + 128 + j for h in range(8) for j in range(64)]
    W["wqb"] = qb[..., nope + pe]
    W["wqbsw"] = _swap_halves(qb, [h * 192 + 128 for h in range(8)])
    kvb = inp["mla_w_kv_b"]
    kn = [h * 256 + j for h in range(8) for j in range(128)]
    vv = [h * 256 + 128 + j for h in range(8) for j in range(128)]
    W["wkvb"] = kvb[..., kn + vv]
    W["wout"] = inp["w_out"]
    return W


_NC_CACHE = {}


def make_in_maps(inp):
    W = prep_weights(inp)
    in_maps = []
    for r in range(8):
        m = {"xT": np.ascontiguousarray(shard_tokens(inp["x"], r).T),
             "consts": make_consts(inp, r), "masks": make_masks(r),
             "pos": np.ascontiguousarray(shard_tokens(inp["positions"], r).reshape(1, NTOK).astype(np.int32))}
        for nm, w in W.items():
            R = w.shape[1]
            m[nm] = np.ascontiguousarray(w[:, r * (R // 8):(r + 1) * (R // 8), :])
        in_maps.append(m)
    return in_maps


def kernel(**inputs):
    inp = {k: np.asarray(v) for k, v in inputs.items()}
    if "all" not in _NC_CACHE:
        _NC_CACHE["all"] = build_program(debug=True)
    nc = _NC_CACHE["all"]
    in_maps = make_in_maps(inp)
    res = run_bass_kernel_spmd(nc, in_maps, core_ids=list(range(8)))
    outs = [np.asarray(r["outT"]) for r in res.results]
    return unshard_out(outs, np.float32)
```

```python
import numpy as np
import ml_dtypes
from contextlib import ExitStack
import concourse.bass as bass
import concourse.mybir as mybir
from concourse.bass_utils import run_bass_kernel_spmd

F32 = mybir.dt.float32
BF16 = mybir.dt.bfloat16
I32 = mybir.dt.int32
AF = mybir.ActivationFunctionType
ALU = mybir.AluOpType

D = 2048
KC = 16
FF = 5632
FC = 44
S = 8192
NTOK = 2048
T = 512
NT = NTOK // T
DEPTH = 2
INC = 3144
EPS = 1e-6
NR = 8
TWO_PI = float(2 * np.pi)
PI = float(np.pi)


class Op:
    __slots__ = ("eng", "fn", "deps", "dma", "chan", "idx", "signal", "val", "bg", "epoch")

    def __init__(self, eng, fn, dma):
        self.eng, self.fn, self.dma = eng, fn, dma
        self.deps = []
        self.chan = None
        self.signal = False
        self.val = 0
        self.bg = False
        self.epoch = 0


class Prog:
    ENGS = ("pe", "act", "dve", "pool", "sp")

    NEPOCH = 8

    def __init__(self, nc, n_dma_chan=14):
        self.nc = nc
        self.epoch = 0
        self.bg_keys = set()
        self.ops = []
        self.last_w = {}
        self.readers = {}
        self.n_dma_chan = n_dma_chan
        self.dma_rr = {e: 0 for e in self.ENGS}
        self.chan_last = {}

    def op(self, eng, fn, reads=(), writes=(), dma=False, bg=False):
        o = Op(eng, fn, dma)
        o.bg = bg
        o.epoch = self.epoch
        o.idx = len(self.ops)
        if bg:
            self.bg_keys.update(writes)
        deps = set()
        for r in reads:
            w = self.last_w.get(r)
            if w is not None:
                deps.add(w)
        for w_ in writes:
            w = self.last_w.get(w_)
            if w is not None:
                deps.add(w)
            for rd in self.readers.get(w_, ()):
                deps.add(rd)
        if eng == "pe":
            deps = {d for d in deps if self.ops[d].eng != "pe"}
        if dma:
            ch = (eng, self.dma_rr[eng] % self.n_dma_chan)
            self.dma_rr[eng] += 1
            o.chan = ch
            prev = self.chan_last.get(ch)
            if prev is not None:
                deps.add(prev)
            self.chan_last[ch] = o.idx
        deps.discard(o.idx)
        o.deps = sorted(deps)
        self.ops.append(o)
        for r in reads:
            self.readers.setdefault(r, []).append(o.idx)
        for w_ in writes:
            self.last_w[w_] = o.idx
            self.readers[w_] = []
        return o.idx

    def barrier(self):
        lasts = {}
        for o in self.ops:
            if o.bg or o.fn is None:
                continue
            lasts[(o.eng, o.chan if o.dma else None)] = o.idx
        tok = sorted(set(lasts.values()))
        self.epoch += 1
        for e in self.ENGS:
            o = Op(e, None, False)
            o.epoch = self.epoch
            o.idx = len(self.ops)
            o.deps = [t for t in tok]
            self.ops.append(o)
        self.last_w = {k: v for k, v in self.last_w.items() if k in self.bg_keys}
        self.readers = {}

    def emit(self, stack):
        nc = self.nc
        ops = self.ops
        for o in ops:
            for d in o.deps:
                ops[d].signal = True
        for o in ops:
            if o.dma:
                o.signal = True
        sems = {}
        for e in self.ENGS:
            for k in range(self.NEPOCH):
                sems[("c", e, k)] = stack.enter_context(nc.semaphore("s_%s%d" % (e, k)))
        for o in ops:
            if o.dma and o.chan not in sems:
                sems[o.chan] = stack.enter_context(nc.semaphore("d_%s_%d" % o.chan))
        cnt = {}
        for o in ops:
            key = o.chan if o.dma else ("c", o.eng, o.epoch % self.NEPOCH)
            if o.signal:
                cnt[key] = cnt.get(key, 0) + (16 if o.dma else 1)
                o.val = cnt[key]
            o.chan = key
        per_eng = {e: [] for e in self.ENGS}
        for o in ops:
            per_eng[o.eng].append(o)
        self.max_vals = cnt
        block = stack.enter_context(nc.Block())
        engmap = {"pe": block.tensor, "act": block.scalar, "dve": block.vector,
                  "pool": block.gpsimd, "sp": block.sync}
        sig_scr = self.sig_scr

        def make(e):
            def body(eng):
                waited = {}
                for o in per_eng[e]:
                    need = {}
                    for d in o.deps:
                        od = ops[d]
                        k = od.chan
                        if od.val > need.get(k, 0):
                            need[k] = od.val
                    for k, v in need.items():
                        if waited.get(k, 0) < v:
                            eng.wait_ge(sems[k], v)
                            waited[k] = v
                    inst = None
                    if o.fn is not None:
                        inst = o.fn(eng)
                    if o.signal:
                        if inst is None:
                            assert e in ("dve", "pool", "act"), e
                            inst = eng.memset(sig_scr[e], 0.0) if e != "act" else eng.activation(
                                out=sig_scr[e], in_=sig_scr[e], func=AF.Copy)
                        inst.then_inc(sems[o.chan], 16 if o.dma else 1)
            return body

        for e in self.ENGS:
            engmap[e](make(e))


_c = [0]


def _alloc(n):
    v = _c[0]
    _c[0] += n
    return v


C_FFN1N = [_alloc(KC) for _ in range(DEPTH)]
C_MIXN = [_alloc(KC) for _ in range(DEPTH)]
C_FFN2N = [_alloc(KC) for _ in range(DEPTH)]
C_FINN = _alloc(KC)
C_QN = [_alloc(4) for _ in range(DEPTH)]
C_KVN = [_alloc(2) for _ in range(DEPTH)]
C_INVF = _alloc(1)
C_SGN = _alloc(1)
C_SINK = [_alloc(4) for _ in range(DEPTH)]
C_FB = [_alloc(1) for _ in range(DEPTH)]
C_OH = _alloc(8)
C_SELA = _alloc(8)
C_SELB = _alloc(8)
C_ID8 = _alloc(8)
NCONST = _c[0]

NMASK = 12
M_TRIL, M_SUP, M_PREV0, M_ID = 8, 9, 10, 11
NEG = -30000.0

O_CQ, O_CKV, O_KR, O_QS, O_KS, O_VS, O_QF, O_KF, O_VF, O_FL = 0, 512, 768, 832, 1344, 1472, 1600, 2112, 2624, 3136
NSW = 704


def build_program(debug=False, nlayers=DEPTH):
    nc = bass.Bass("TRN2", target_bir_lowering=False)
    st = ExitStack()
    P = Prog(nc)

    def din(name, shape, dt):
        return nc.dram_tensor(name, list(shape), dt, kind="ExternalInput").ap()

    def dscr(name, shape, dt):
        return nc.dram_tensor(name, list(shape), dt).ap()

    xT_in = din("xT", [D, NTOK], F32)
    consts = din("consts", [128, NCONST], F32)
    masks_in = din("masks", [128, NMASK * 128], BF16)
    pos_in = din("pos", [1, NTOK], I32)
    outT = nc.dram_tensor("outT", [D, NTOK], F32, kind="ExternalOutput").ap()
    dbg_mixed = dbg_x = None
    if debug:
        dbg_mixed = nc.dram_tensor("dbg_mixed", [D, NTOK], BF16, kind="ExternalOutput").ap()
        dbg_x = nc.dram_tensor("dbg_x", [D, NTOK], F32, kind="ExternalOutput").ap()

    WSPEC = [("wg1", D, FF), ("wu1", D, FF), ("wd1", FF, D), ("win", D, INC), ("winsw", D, NSW),
             ("wqb", 512, 1536), ("wqbsw", 512, 512), ("wkvb", 256, 2048), ("wout", D, D),
             ("wg2", D, FF), ("wu2", D, FF), ("wd2", FF, D)]
    w_in_d, w_sh, w_b = {}, {}, {}
    for nm, R, C in WSPEC:
        w_in_d[nm] = din(nm, [DEPTH, R // NR, C], F32)
        for l in range(DEPTH):
            w_sh[nm, l] = dscr("sh_%s_%d" % (nm, l), [R // NR, C], BF16)
            w_b[nm, l] = dscr("wb_%s_%d" % (nm, l), [R, C], BF16)

    xres = dscr("xres", [D, NTOK], F32)
    QnT_d = [dscr("QnT%d" % l, [1024, NTOK], BF16) for l in range(DEPTH)]
    QpT_d = [dscr("QpT%d" % l, [512, NTOK], BF16) for l in range(DEPTH)]
    Qs_d = [dscr("Qs%d" % l, [512, NTOK], BF16) for l in range(DEPTH)]
    Qf_d = [dscr("Qf%d" % l, [8, 67, NTOK], BF16) for l in range(DEPTH)]
    NB = NTOK // 2
    GSPEC = [("Kn", 1024, NB, BF16), ("Kpe", 64, NB, BF16), ("Vm", NB, 1024, BF16), ("Ks", 128, NB, BF16),
             ("Vs", NB, 128, BF16), ("Kf", 512, NB, BF16), ("Vf", NB, 512, BF16), ("FL", 8, NB, F32)]
    g_in, g_out = {}, {}
    for nm, R, C, dt in GSPEC:
        for l in range(DEPTH):
            for b_ in range(2):
                g_in[nm, l, b_] = dscr("gi_%s_%d_%d" % (nm, l, b_), [R, C], dt)
                g_out[nm, l, b_] = dscr("go_%s_%d_%d" % (nm, l, b_), [NR * R, C], dt)

    def sb(name, shape, dt):
        return st.enter_context(nc.sbuf_tensor(name, list(shape), dt))

    cst = sb("cst", [128, NCONST], F32)
    mk = sb("mk", [128, NMASK, 128], BF16)
    xT = sb("xTt", [128, KC, T], F32)
    hT = sb("hT", [128, KC, T], BF16)
    aT = sb("aT", [128, FC, T], BF16)
    wgu = sb("wgu", [128, 4, KC, 256], BF16)
    wd = sb("wd", [128, 2, FC, 256], BF16)
    ones = sb("ones", [128, 128], BF16)
    tmpA = sb("tmpA", [128, 2, T], F32)
    rstd = sb("rstd", [128, T], F32)
    swa_s = sb("swa_s", [128, 6144 // 2 * 2], BF16)
    cpT = sb("cpT", [128, 2, 8, 8, 8], F32)
    small = sb("small", [128, 64], F32)
    sigp = sb("sigp", [128, 8], F32)
    ps = [st.enter_context(nc.psum_tensor("ps%d" % i, [128, 512], F32)) for i in range(8)]
    P.sig_scr = {"dve": sigp[0:1, 0:1], "pool": sigp[0:1, 1:2], "act": sigp[0:1, 2:3]}

    aflat = aT[:].rearrange("p f t -> p (f t)")
    wflat = wd[:].rearrange("p s f c -> p (s f c)")

    def view(flat, byte_off, nelem, dt, pattern=None, **kw):
        if dt == BF16:
            v = flat[:, byte_off // 2: byte_off // 2 + nelem]
        else:
            v = flat[:, byte_off // 2: byte_off // 2 + 2 * nelem].bitcast(dt)
        if pattern:
            v = v.rearrange(pattern, **kw)
        return v

    cq32 = view(aflat, 0, 2048, F32, "p (a t) -> p a t", a=4)
    ckv32 = view(aflat, 8192, 1024, F32, "p (a t) -> p a t", a=2)
    cqn = view(aflat, 12288, 2048, BF16, "p (a t) -> p a t", a=4)
    ckvn = view(aflat, 16384, 1024, BF16, "p (a t) -> p a t", a=2)
    sqs = view(aflat, 18432, 2048, BF16, "p (a t) -> p a t", a=4)
    cos2 = view(aflat, 22528, 512, F32)
    sin2 = view(aflat, 24576, 512, F32)
    ang = view(aflat, 26624, 512, F32)
    angi = view(aflat, 28672, 512, I32)
    posb = view(aflat, 30720, 512, I32)
    stage = view(aflat, 32768, 1024, BF16, "p (a t) -> p a t", a=2)
    stage32 = view(aflat, 34816, 512, F32)
    t1 = view(aflat, 36864, 512, F32)
    t2 = view(aflat, 38912, 512, F32)
    ang2 = view(aflat, 40960, 512, F32)
    FLs = view(aflat, 0, 8192, F32)
    FLs4 = FLs.rearrange("p (m r q) -> p m r q", m=8, r=8)
    cown = view(aflat, 32768, 1024, F32)
    cneg = view(aflat, 36864, 1024, F32)
    KpeAll = view(aflat, 0, 8192, BF16, "p (r t) -> p r t", r=8)
    selG = view(aflat, 16384, 8192, BF16, "p (r t) -> p r t", r=8)
    selacc = view(aflat, 32768, 1024, F32)
    Kc = view(wflat, 0, 3072, BF16, "p (s t) -> p s t", s=3)
    Vc = view(wflat, 6144, 3072, BF16, "p (s m d) -> p s m d", s=3, m=8)
    Qn = view(wflat, 12288, 1024, BF16, "p (s t) -> p s t", s=2)
    Qp = view(wflat, 14336, 1024, BF16, "p (s t) -> p s t", s=2)
    Pb = view(wflat, 16384, 2048, BF16, "p (s t) -> p s t", s=4)
    qs_sb = view(wflat, 20480, 4096, BF16, "p (h t) -> p h t", h=8)
    cq3v = view(wflat, 32768, 3072, BF16, "p (j t) -> p j t", j=3)
    r1 = view(aflat, 40960, 1024, F32)
    Cs = view(wflat, 0, 8192, F32)
    Cs4 = Cs.rearrange("p (m r q) -> p m r q", m=8, r=8)
    wqb_s = view(wflat, 0, 6144, BF16, "p (k c) -> p k c", k=4)
    wqbsw_s = view(wflat, 12288, 2048, BF16, "p (k c) -> p k c", k=4)
    wkvb_s = view(wflat, 16384, 4096, BF16, "p (k c) -> p k c", k=2)
    sflat = swa_s[:]
    KsPrev = sflat[:, 0:2048].rearrange("p (g t) -> p g t", g=2)
    KsOwn = sflat[:, 2048:4096].rearrange("p (g t) -> p g t", g=2)
    VsPrev = sflat[:, 4096:5120].rearrange("p (m d) -> p m d", m=8)
    VsOwn = sflat[:, 5120:6144].rearrange("p (m d) -> p m d", m=8)

    def dma(eng, out, in_, reads=(), writes=()):
        return P.op(eng, lambda e: e.dma_start(out=out, in_=in_), reads=reads, writes=writes, dma=True)

    dma("sp", cst[:], consts, writes=["cst"])
    dma("sp", mk[:].rearrange("p a b -> p (a b)"), masks_in, writes=["mk"])
    P.op("dve", lambda e: e.memset(ones[:], 1.0), writes=["ones"])
    P.op("dve", lambda e: e.memset(sigp[:], 0.0), writes=["sigp"])

    ccsem = st.enter_context(nc.semaphore("ccsem"))
    cc_n = [0]

    def allgather(src, dst, reads, writes, bg=False):
        cc_n[0] += 1
        n = cc_n[0]

        def fn(e):
            e.collective_compute("AllGather", ALU.bypass, replica_groups=[list(range(NR))],
                                 ins=[src.opt()], outs=[dst.opt()]).then_inc(ccsem)
            e.wait_ge(ccsem, n)
            return e.memset(sigp[0:1, 3:4], 0.0)
        P.op("pool", fn, reads=reads, writes=writes, bg=bg)

    def conv_weight(nm, l):
        R, C = [(r, c) for (n_, r, c) in WSPEC if n_ == nm][0]
        src = w_in_d[nm][l]
        dst = w_sh[nm, l]
        c = C
        for cand in (2048, 1572, 1536, 1408, 1024, 704, 512):
            if C % cand == 0 and cand <= 2048:
                c = cand
                break
        s_ap = src.rearrange("a (b c) -> (a b) c", c=c)
        d_ap = dst.rearrange("a (b c) -> (a b) c", c=c)
        P.op("pool", lambda e: e.dma_start(out=d_ap, in_=s_ap), writes=[("sh", nm, l)], dma=True, bg=True)
        allgather(dst, w_b[nm, l], reads=[("sh", nm, l)], writes=[("wb", nm, l)], bg=True)

    def rmsnorm_to_hT(gcol, out_f32=None):
        P.op("act", lambda e: e.activation(out=aT[:, 0:KC, :], in_=xT[:], func=AF.Square),
             reads=["xT"], writes=["aT"] + [("aTc", f) for f in range(KC)])

        def ssmm(e):
            for k in range(KC):
                i = e.matmul(ps[6][:], lhsT=ones[:], rhs=aT[:, k, :], start=(k == 0), stop=(k == KC - 1))
            return i
        P.op("pe", ssmm, reads=["aT", "ones"] + [("aTc", f) for f in range(KC)], writes=[("ps", 6)])
        rstd_from_ss(6, 1.0 / D)
        if out_f32 is None:
            def mkh(e):
                for k in range(KC):
                    i = e.scalar_tensor_tensor(out=hT[:, k, :], in0=xT[:, k, :], scalar=cst[:, gcol + k:gcol + k + 1],
                                               in1=rstd[:], op0=ALU.mult, op1=ALU.mult)
                return i
            P.op("dve", mkh, reads=["xT", "rstd", "cst"], writes=["hT"])
        else:
            def mkf(e):
                for k in range(KC):
                    i = e.scalar_tensor_tensor(out=xT[:, k, :], in0=xT[:, k, :], scalar=cst[:, gcol + k:gcol + k + 1],
                                               in1=rstd[:], op0=ALU.mult, op1=ALU.mult)
                return i
            P.op("dve", mkf, reads=["xT", "rstd", "cst"], writes=["xT"])

    def rstd_from_ss(bank, inv_n):
        P.op("dve", lambda e: e.tensor_scalar(out=tmpA[:, 0, :], in0=ps[bank][:], scalar1=inv_n, scalar2=EPS,
                                              op0=ALU.mult, op1=ALU.add),
             reads=[("ps", bank)], writes=["tmpA0"])
        P.op("dve", lambda e: e.reciprocal(out=tmpA[:, 1, :], in_=tmpA[:, 0, :]), reads=["tmpA0"], writes=["tmpA1"])
        P.op("act", lambda e: e.activation(out=rstd[:], in_=tmpA[:, 1, :], func=AF.Sqrt),
             reads=["tmpA1"], writes=["rstd"])

    wslot = [0]
    dslot = [0]
    pref = {}

    def wblock(src, c0, n, key, name, prefetch=False):
        k = (name, c0, n)
        if not prefetch and k in pref:
            return pref.pop(k)
        s_ = wslot[0] % 4
        wslot[0] += 1
        dma("sp", wgu[:, s_, :, 0:n], src[:, :, c0:c0 + n], reads=[key], writes=[("wgu", s_)])
        if prefetch:
            pref[k] = s_
        return s_

    def ffn_views(i, l):
        sfx = "1" if i == 0 else "2"
        return (sfx, w_b["wg" + sfx, l].rearrange("(k p) c -> p k c", p=128),
                w_b["wu" + sfx, l].rearrange("(k p) c -> p k c", p=128))

    def prefetch_ffn(i, l):
        sfx, wg_d, wu_d = ffn_views(i, l)
        for fb in range(2):
            wblock(wg_d, fb * 256, 256, ("wb", "wg" + sfx, l), ("wg" + sfx, l), prefetch=True)
            wblock(wu_d, fb * 256, 256, ("wb", "wu" + sfx, l), ("wu" + sfx, l), prefetch=True)

    def prefetch_win(l):
        win_d = w_b["win", l].rearrange("(k p) c -> p k c", p=128)
        for c0, n in ((O_CQ, 256), (O_CQ + 256, 256), (O_CKV, 256), (O_KR, 64)):
            wblock(win_d, c0, n, ("wb", "win", l), ("win", l), prefetch=True)

    def prefetch_wout(l):
        wo_d = w_b["wout", l].rearrange("(k p) c -> p k c", p=128)
        for blk in range(4):
            wblock(wo_d, blk * 256, 256, ("wb", "wout", l), ("wout", l), prefetch=True)

    def ffn(i, l):
        sfx = "1" if i == 0 else "2"
        wg_d = w_b["wg" + sfx, l].rearrange("(k p) c -> p k c", p=128)
        wu_d = w_b["wu" + sfx, l].rearrange("(k p) c -> p k c", p=128)
        wd_d = w_b["wd" + sfx, l].rearrange("(f p) c -> p f c", p=128)
        for fb in range(FC // 2):
            sg = wblock(wg_d, fb * 256, 256, ("wb", "wg" + sfx, l), ("wg" + sfx, l))
            su = wblock(wu_d, fb * 256, 256, ("wb", "wu" + sfx, l), ("wu" + sfx, l))
            for cc in range(2):
                f = fb * 2 + cc
                pb = f % 2

                def gmm(e, sg=sg, cc=cc, pb=pb):
                    for k in range(KC):
                        ii = e.matmul(ps[pb][:], lhsT=wgu[:, sg, k, cc * 128:(cc + 1) * 128], rhs=hT[:, k, :],
                                      start=(k == 0), stop=(k == KC - 1))
                    return ii

                def umm(e, su=su, cc=cc, pb=pb):
                    for k in range(KC):
                        ii = e.matmul(ps[2 + pb][:], lhsT=wgu[:, su, k, cc * 128:(cc + 1) * 128], rhs=hT[:, k, :],
                                      start=(k == 0), stop=(k == KC - 1))
                    return ii
                P.op("pe", gmm, reads=[("wgu", sg), "hT"], writes=[("ps", pb)])
                P.op("pe", umm, reads=[("wgu", su), "hT"], writes=[("ps", 2 + pb)])
                P.op("act", lambda e, pb=pb: e.activation(out=tmpA[:, pb, :], in_=ps[pb][:], func=AF.Silu),
                     reads=[("ps", pb)], writes=["tmpA%d" % pb])
                P.op("dve", lambda e, pb=pb, f=f: e.tensor_tensor(out=aT[:, f, :], in0=ps[2 + pb][:], in1=tmpA[:, pb, :], op=ALU.mult),
                     reads=[("ps", 2 + pb), "tmpA%d" % pb], writes=[("aTc", f)])
        aT_all = [("aTc", f) for f in range(FC)]
        for dp in range(KC // 2):
            slot = dslot[0] % 2
            dslot[0] += 1
            dma("sp", wd[:, slot, :, :], wd_d[:, :, dp * 256:(dp + 1) * 256], reads=[("wb", "wd" + sfx, l)], writes=[("wd", slot)])
            for cc in range(2):
                c = dp * 2 + cc
                pb = 4 + (c % 2)

                def dmm(e, slot=slot, cc=cc, pb=pb):
                    for f in range(FC):
                        ii = e.matmul(ps[pb][:], lhsT=wd[:, slot, f, cc * 128:(cc + 1) * 128], rhs=aT[:, f, :],
                                      start=(f == 0), stop=(f == FC - 1))
                    return ii
                P.op("pe", dmm, reads=[("wd", slot)] + aT_all, writes=[("ps", pb)])
                P.op("dve", lambda e, c=c, pb=pb: e.scalar_tensor_tensor(out=xT[:, c, :], in0=ps[pb][:], scalar=0.5,
                                                                         in1=xT[:, c, :], op0=ALU.mult, op1=ALU.add),
                     reads=[("ps", pb), "xT"], writes=["xT"])

    def rope_tables(tt):
        dma("sp", posb, pos_in[:, tt * T:(tt + 1) * T].partition_broadcast(128), writes=["posb"])
        P.op("dve", lambda e: e.tensor_copy(out=ang, in_=posb), reads=["posb"], writes=["ang"])
        P.op("dve", lambda e: e.tensor_scalar(out=ang, in0=ang, scalar1=cst[:, C_INVF:C_INVF + 1], scalar2=None, op0=ALU.mult),
             reads=["ang", "cst"], writes=["ang"])

        def reduce_sin(src, dst, key):
            P.op("dve", lambda e: e.tensor_scalar(out=angi, in0=src, scalar1=1.0 / TWO_PI, scalar2=None, op0=ALU.mult),
                 reads=[key], writes=["angi"])
            P.op("dve", lambda e: e.tensor_copy(out=t1, in_=angi), reads=["angi"], writes=["t1"])
            P.op("dve", lambda e: e.scalar_tensor_tensor(out=t2, in0=t1, scalar=-TWO_PI, in1=src, op0=ALU.mult, op1=ALU.add),
                 reads=["t1", key], writes=["t2"])
            P.op("dve", lambda e: e.tensor_scalar(out=t2, in0=t2, scalar1=-PI, scalar2=PI, op0=ALU.max, op1=ALU.min),
                 reads=["t2"], writes=["t2"])
            P.op("act", lambda e: e.activation(out=dst, in_=t2, func=AF.Sin), reads=["t2"], writes=[key + "_o"])
        reduce_sin(ang, sin2, "ang")
        P.op("dve", lambda e: e.tensor_scalar(out=sin2, in0=sin2, scalar1=cst[:, C_SGN:C_SGN + 1], scalar2=None, op0=ALU.mult),
             reads=["ang_o", "cst"], writes=["sin2"])
        P.op("dve", lambda e: e.tensor_scalar(out=ang2, in0=ang, scalar1=PI / 2, scalar2=None, op0=ALU.add),
             reads=["ang"], writes=["ang2"])
        reduce_sin(ang2, cos2, "ang2")
        P.op("dve", lambda e: e.tensor_copy(out=cos2, in_=cos2), reads=["ang2_o"], writes=["cos2"])

    def projections(tt, l):
        win_d = w_b["win", l].rearrange("(k p) c -> p k c", p=128)
        winsw_d = w_b["winsw", l].rearrange("(k p) c -> p k c", p=128)
        tsl = slice(tt * T, (tt + 1) * T)
        bb_ = tt // 2
        lsl = slice((tt % 2) * T, (tt % 2 + 1) * T)
        lo = (tt % 2) * T
        pbank = [0]
        stg = [0]

        def load_w(src, c0, n, key):
            return wblock(src, c0, n, key, (key[1], l))

        def mm_fm(s, coff, ncols, bank, rhs=hT, rkey="hT", nk=KC):
            def fn(e):
                for k in range(nk):
                    ii = e.matmul(ps[bank][0:ncols, :], lhsT=wgu[:, s, k, coff:coff + ncols], rhs=rhs[:, k, :],
                                  start=(k == 0), stop=(k == nk - 1))
                return ii
            P.op("pe", fn, reads=[("wgu", s), rkey], writes=[("ps", bank)])

        def nextbank():
            b_ = pbank[0] % 6
            pbank[0] += 1
            return b_

        def nextstage():
            s_ = stg[0] % 2
            stg[0] += 1
            return s_

        def store_fm(bank, nrows, dst, scale=None, wkey=None, f32=False):
            if f32:
                P.op("act", lambda e: e.activation(out=stage32[0:nrows, :], in_=ps[bank][0:nrows, :], func=AF.Copy),
                     reads=[("ps", bank)], writes=["stage32"])
                dma("sp", dst, stage32[0:nrows, :], reads=["stage32"], writes=[wkey])
                return
            s_ = nextstage()
            if scale is None:
                P.op("act", lambda e: e.activation(out=stage[0:nrows, s_, :], in_=ps[bank][0:nrows, :], func=AF.Copy),
                     reads=[("ps", bank)], writes=[("stage", s_)])
            else:
                P.op("dve", lambda e: e.tensor_scalar(out=stage[0:nrows, s_, :], in0=ps[bank][0:nrows, :], scalar1=scale, scalar2=None, op0=ALU.mult),
                     reads=[("ps", bank)], writes=[("stage", s_)])
            dma("sp", dst, stage[0:nrows, s_, :], reads=[("stage", s_)], writes=[wkey])

        def rope_store(bank_a, bank_b, nrows, dst, scale, wkey):
            s_ = nextstage()
            P.op("dve", lambda e: e.scalar_tensor_tensor(out=t1[0:nrows, :], in0=ps[bank_a][0:nrows, :], scalar=scale,
                                                         in1=cos2[0:nrows, :], op0=ALU.mult, op1=ALU.mult),
                 reads=[("ps", bank_a), "cos2"], writes=["t1"])
            P.op("dve", lambda e: e.scalar_tensor_tensor(out=t2[0:nrows, :], in0=ps[bank_b][0:nrows, :], scalar=scale,
                                                         in1=sin2[0:nrows, :], op0=ALU.mult, op1=ALU.mult),
                 reads=[("ps", bank_b), "sin2"], writes=["t2"])
            P.op("dve", lambda e: e.tensor_tensor(out=stage[0:nrows, s_, :], in0=t1[0:nrows, :], in1=t2[0:nrows, :], op=ALU.add),
                 reads=["t1", "t2"], writes=[("stage", s_)])
            dma("sp", dst, stage[0:nrows, s_, :], reads=[("stage", s_)], writes=[wkey])

        kw = ("wb", "win", l)
        kws = ("wb", "winsw", l)
        for blk in range(2):
            s = load_w(win_d, O_CQ + blk * 256, 256, kw)
            for cc in range(2):
                b_ = nextbank()
                mm_fm(s, cc * 128, 128, b_)
                ch = blk * 2 + cc
                P.op("act", lambda e, b_=b_, ch=ch: e.activation(out=cq32[:, ch, :], in_=ps[b_][:], func=AF.Copy),
                     reads=[("ps", b_)], writes=[("cq32", ch)])
        s = load_w(win_d, O_CKV, 256, kw)
        for cc in range(2):
            b_ = nextbank()
            mm_fm(s, cc * 128, 128, b_)
            P.op("act", lambda e, b_=b_, cc=cc: e.activation(out=ckv32[:, cc, :], in_=ps[b_][:], func=AF.Copy),
                 reads=[("ps", b_)], writes=[("ckv32", cc)])
        s = load_w(win_d, O_KR, 64, kw)
        s2 = load_w(winsw_d, 0, 64, kws)
        ba, bb = nextbank(), nextbank()
        mm_fm(s, 0, 64, ba)
        mm_fm(s2, 0, 64, bb)
        rope_store(ba, bb, 64, g_in["Kpe", l, bb_][:, lsl], 1.0, ("gi", "Kpe", l, tt))
        for blk in range(2):
            s = load_w(win_d, O_QS + blk * 256, 256, kw)
            s2 = load_w(winsw_d, 64 + blk * 256, 256, kws)
            for cc in range(2):
                ba, bb = nextbank(), nextbank()
                mm_fm(s, cc * 128, 128, ba)
                mm_fm(s2, cc * 128, 128, bb)
                ch = blk * 2 + cc
                rope_store(ba, bb, 128, Qs_d[l][ch * 128:(ch + 1) * 128, tsl], 0.125, ("Qs", l, tt, ch))
        s = load_w(win_d, O_KS, 128, kw)
        s2 = load_w(winsw_d, 576, 128, kws)
        ba, bb = nextbank(), nextbank()
        mm_fm(s, 0, 128, ba)
        mm_fm(s2, 0, 128, bb)
        rope_store(ba, bb, 128, g_in["Ks", l, bb_][:, lsl], 1.0, ("gi", "Ks", l, tt))
        for blk in range(2):
            s = load_w(win_d, O_QF + blk * 256, 256, kw)
            for cc in range(2):
                b_ = nextbank()
                mm_fm(s, cc * 128, 128, b_)
                ch = blk * 2 + cc
                store_fm(b_, 128, Qf_d[l][2 * ch:2 * ch + 2, 0:64, tsl], scale=0.125, wkey=("Qf", l, tt, ch))
        for blk in range(2):
            s = load_w(win_d, O_KF + blk * 256, 256, kw)
            for cc in range(2):
                b_ = nextbank()
                mm_fm(s, cc * 128, 128, b_)
                ch = blk * 2 + cc
                store_fm(b_, 128, g_in["Kf", l, bb_][ch * 128:(ch + 1) * 128, lsl], wkey=("gi", "Kf", l, tt, ch))
        s = load_w(win_d, O_FL, 8, kw)
        b_ = nextbank()
        mm_fm(s, 0, 8, b_)
        store_fm(b_, 8, g_in["FL", l, bb_][:, lsl], wkey=("gi", "FL", l, tt), f32=True)
        sv = load_w(win_d, O_VS, 128, kw)
        for tb in range(4):
            b_ = nextbank()

            def fn(e, tb=tb, b_=b_):
                for k in range(KC):
                    ii = e.matmul(ps[b_][:, 0:128], lhsT=hT[:, k, tb * 128:(tb + 1) * 128], rhs=wgu[:, sv, k, 0:128],
                                  start=(k == 0), stop=(k == KC - 1))
                return ii
            P.op("pe", fn, reads=[("wgu", sv), "hT"], writes=[("ps", b_)])
            s_ = nextstage()
            P.op("act", lambda e, b_=b_, s_=s_: e.activation(out=stage[:, s_, 0:128], in_=ps[b_][:, 0:128], func=AF.Copy),
                 reads=[("ps", b_)], writes=[("stage", s_)])
            dma("sp", g_in["Vs", l, bb_][lo + tb * 128: lo + (tb + 1) * 128, :], stage[:, s_, 0:128],
                reads=[("stage", s_)], writes=[("gi", "Vs", l, tt, tb)])
        sva = load_w(win_d, O_VF, 256, kw)
        svb = load_w(win_d, O_VF + 256, 256, kw)
        for tb in range(4):
            b_ = nextbank()

            def fn(e, tb=tb, b_=b_):
                for half, sx in ((0, sva), (1, svb)):
                    for k in range(KC):
                        ii = e.matmul(ps[b_][:, half * 256:(half + 1) * 256], lhsT=hT[:, k, tb * 128:(tb + 1) * 128],
                                      rhs=wgu[:, sx, k, 0:256], start=(k == 0), stop=(k == KC - 1))
                return ii
            P.op("pe", fn, reads=[("wgu", sva), ("wgu", svb), "hT"], writes=[("ps", b_)])
            s_ = nextstage()
            P.op("act", lambda e, b_=b_, s_=s_: e.activation(out=stage[:, s_, :], in_=ps[b_][:], func=AF.Copy),
                 reads=[("ps", b_)], writes=[("stage", s_)])
            dma("sp", g_in["Vf", l, bb_][lo + tb * 128: lo + (tb + 1) * 128, :], stage[:, s_, :],
                reads=[("stage", s_)], writes=[("gi", "Vf", l, tt, tb)])

        def small_norm(src32, nch, srckey, dstn, dstkey, gcol, nfeat):
            P.op("act", lambda e: e.activation(out=sqs[:, 0:nch, :], in_=src32[:, 0:nch, :], func=AF.Square),
                 reads=[(srckey, c_) for c_ in range(nch)], writes=["sqs"])

            def ssmm(e):
                for k in range(nch):
                    i = e.matmul(ps[6][:], lhsT=ones[:], rhs=sqs[:, k, :], start=(k == 0), stop=(k == nch - 1))
                return i
            P.op("pe", ssmm, reads=["sqs", "ones"], writes=[("ps", 6)])
            rstd_from_ss(6, 1.0 / nfeat)

            def mkn(e):
                for k in range(nch):
                    i = e.scalar_tensor_tensor(out=dstn[:, k, :], in0=src32[:, k, :], scalar=cst[:, gcol + k:gcol + k + 1],
                                               in1=rstd[:], op0=ALU.mult, op1=ALU.mult)
                return i
            P.op("dve", mkn, reads=[(srckey, c_) for c_ in range(nch)] + ["rstd", "cst"], writes=[dstkey])
        small_norm(cq32, 4, "cq32", cqn, "cqn", C_QN[l], 512)
        QSC = float(192 ** -0.5)
        for h in range(8):
            b_ = nextbank()

            def fn(e, h=h, b_=b_):
                for k in range(4):
                    ii = e.matmul(ps[b_][:], lhsT=wqb_s[:, k, h * 128:(h + 1) * 128], rhs=cqn[:, k, :], start=(k == 0), stop=(k == 3))
                return ii
            P.op("pe", fn, reads=[("wqb_s", l), "cqn"], writes=[("ps", b_)])
            store_fm(b_, 128, QnT_d[l][h * 128:(h + 1) * 128, tsl], scale=QSC, wkey=("QnT", l, tt, h))
        for hp in range(4):
            ba, bb = nextbank(), nextbank()

            def fa(e, hp=hp, ba=ba):
                for k in range(4):
                    ii = e.matmul(ps[ba][:], lhsT=wqb_s[:, k, 1024 + hp * 128:1024 + (hp + 1) * 128], rhs=cqn[:, k, :], start=(k == 0), stop=(k == 3))
                return ii

            def fb_(e, hp=hp, bb=bb):
                for k in range(4):
                    ii = e.matmul(ps[bb][:], lhsT=wqbsw_s[:, k, hp * 128:(hp + 1) * 128], rhs=cqn[:, k, :], start=(k == 0), stop=(k == 3))
                return ii
            P.op("pe", fa, reads=[("wqb_s", l), "cqn"], writes=[("ps", ba)])
            P.op("pe", fb_, reads=[("wqb_s", l), "cqn"], writes=[("ps", bb)])
            rope_store(ba, bb, 128, QpT_d[l][hp * 128:(hp + 1) * 128, tsl], QSC, ("QpT", l, tt, hp))
        small_norm(ckv32, 2, "ckv32", ckvn, "ckvn", C_KVN[l], 256)
        for h in range(8):
            b_ = nextbank()

            def fn(e, h=h, b_=b_):
                for k in range(2):
                    ii = e.matmul(ps[b_][:], lhsT=wkvb_s[:, k, h * 128:(h + 1) * 128], rhs=ckvn[:, k, :], start=(k == 0), stop=(k == 1))
                return ii
            P.op("pe", fn, reads=[("wkvb_s", l), "ckvn"], writes=[("ps", b_)])
            store_fm(b_, 128, g_in["Kn", l, bb_][h * 128:(h + 1) * 128, lsl], wkey=("gi", "Kn", l, tt, h))
        for tb in range(4):
            for half in range(2):
                b_ = nextbank()

                def fn(e, tb=tb, half=half, b_=b_):
                    for k in range(2):
                        ii = e.matmul(ps[b_][:], lhsT=ckvn[:, k, tb * 128:(tb + 1) * 128],
                                      rhs=wkvb_s[:, k, 1024 + half * 512:1024 + (half + 1) * 512], start=(k == 0), stop=(k == 1))
                    return ii
                P.op("pe", fn, reads=[("wkvb_s", l), "ckvn"], writes=[("ps", b_)])
                s_ = nextstage()
                P.op("act", lambda e, b_=b_, s_=s_: e.activation(out=stage[:, s_, :], in_=ps[b_][:], func=AF.Copy),
                     reads=[("ps", b_)], writes=[("stage", s_)])
                dma("sp", g_in["Vm", l, bb_][lo + tb * 128: lo + (tb + 1) * 128, half * 512:(half + 1) * 512], stage[:, s_, :],
                    reads=[("stage", s_)], writes=[("gi", "Vm", l, tt, tb, half)])

    def load_layer_small_weights(l):
        dma("sp", wqb_s, w_b["wqb", l].rearrange("(k p) c -> p k c", p=128), reads=[("wb", "wqb", l)], writes=[("wqb_s", l)])
        dma("sp", wqbsw_s, w_b["wqbsw", l].rearrange("(k p) c -> p k c", p=128), reads=[("wb", "wqbsw", l)], writes=[("wqb_s", l)])
        dma("sp", wkvb_s, w_b["wkvb", l].rearrange("(k p) c -> p k c", p=128), reads=[("wb", "wkvb", l)], writes=[("wkvb_s", l)])

    def phase_a_tile(tt, l, next_ffn=None):
        rmsnorm_to_hT(C_FFN1N[l])
        ffn(0, l)
        prefetch_win(l)
        rmsnorm_to_hT(C_MIXN[l])
        xres_v = xres.rearrange("(k p) n -> p k n", p=128)
        dma("sp", xres_v[:, :, tt * T:(tt + 1) * T], xT[:], reads=["xT"], writes=[("xres", tt)])
        P.barrier()
        rope_tables(tt)
        load_layer_small_weights(l)
        projections(tt, l)
        if next_ffn is not None:
            prefetch_ffn(*next_ffn)
        P.barrier()

    def gather_all(l, b):
        for nm, R, C, dt in GSPEC:
            allgather(g_in[nm, l, b], g_out[nm, l, b], reads=[], writes=[("go", nm, l, b)], bg=True)

    def fox_c_prep(l, b):
        FLg = g_out["FL", l, b].rearrange("(r h) n -> r h n", r=NR)
        for rr in range(NR):
            dma("sp", FLs4[0:8, :, rr, :], FLg[rr, :, :].rearrange("h (m q) -> h m q", m=8),
                reads=[("go", "FL", l, b)], writes=[("FLs", rr)])
        fl_all = [("FLs", rr) for rr in range(NR)]
        P.op("dve", lambda e: e.tensor_scalar(out=small[0:8, 0:1], in0=cst[0:8, C_FB[l]:C_FB[l] + 1], scalar1=-1.0, scalar2=None, op0=ALU.mult),
             reads=["cst"], writes=["nfb"])
        P.op("act", lambda e: e.activation(out=FLs[0:8, :], in_=FLs[0:8, :], func=AF.Exp, bias=small[0:8, 0:1], scale=-1.0),
             reads=fl_all + ["nfb"], writes=["FLs"])
        P.op("act", lambda e: e.activation(out=FLs[0:8, :], in_=FLs[0:8, :], func=AF.Ln, bias=1.0),
             reads=["FLs"], writes=["FLs"])
        P.op("dve", lambda e: e.tensor_tensor_scan(out=Cs[0:8, :], data0=FLs[0:8, :], data1=FLs[0:8, :], initial=0.0,
                                                  op0=ALU.add, op1=ALU.max),
             reads=["FLs"], writes=["Cs"])
        def tr(e):
            for m in range(8):
                for rr in range(8):
                    g = m * 8 + rr
                    ii = e.transpose(out=ps[7][:, (rr * 8 + m) * 8:(rr * 8 + m) * 8 + 8], in_=Cs[0:8, g * 128:(g + 1) * 128],
                                     identity=cst[0:8, C_ID8:C_ID8 + 8])
            return ii
        P.op("pe", tr, reads=["Cs", "cst"], writes=[("ps", 7)])
        P.op("dve", lambda e: e.tensor_copy(out=cpT[:, b, :, :, :].rearrange("p r m h -> p (r m h)"), in_=ps[7][:]),
             reads=[("ps", 7)], writes=[("cpT", b)])
        cown3 = cown[0:8, :].rearrange("p (m q) -> p m q", m=8)
        P.op("dve", lambda e: e.tensor_scalar(out=cown3, in0=Cs4[0:8, :, 0, :], scalar1=cst[0:8, C_OH:C_OH + 1], scalar2=None, op0=ALU.mult),
             reads=["Cs", "cst"], writes=["cown"])
        for rr in range(1, 8):
            P.op("dve", lambda e, rr=rr: e.scalar_tensor_tensor(out=cown3, in0=Cs4[0:8, :, rr, :], scalar=cst[0:8, C_OH + rr:C_OH + rr + 1],
                                                                 in1=cown3, op0=ALU.mult, op1=ALU.add),
                 reads=["Cs", "cown"], writes=["cown"])
        P.op("dve", lambda e: e.tensor_scalar(out=cneg[0:8, :], in0=cown[0:8, :], scalar1=-1.0, scalar2=None, op0=ALU.mult),
             reads=["cown"], writes=["cneg"])
        P.op("dve", lambda e: e.tensor_copy(out=cq3v[0:8, 0, :], in_=cneg[0:8, :]), reads=["cneg"], writes=["cq3a"])
        P.op("dve", lambda e: e.tensor_tensor(out=r1[0:8, :], in0=cneg[0:8, :], in1=cq3v[0:8, 0, :], op=ALU.subtract),
             reads=["cneg", "cq3a"], writes=["r1"])
        P.op("dve", lambda e: e.tensor_copy(out=cq3v[0:8, 1, :], in_=r1[0:8, :]), reads=["r1"], writes=["cq3b"])
        P.op("dve", lambda e: e.tensor_tensor(out=cneg[0:8, :], in0=r1[0:8, :], in1=cq3v[0:8, 1, :], op=ALU.subtract),
             reads=["r1", "cq3b"], writes=["cneg"])
        P.op("dve", lambda e: e.tensor_copy(out=cq3v[0:8, 2, :], in_=cneg[0:8, :]), reads=["cneg"], writes=["cq3c"])
        dma("sp", Qf_d[l][:, 64:67, b * 1024:(b + 1) * 1024], cq3v[0:8, :, :], reads=["cq3a", "cq3b", "cq3c"],
            writes=[("Qfc", l, b)])

    def swa_select(l, b):
        Ksg = g_out["Ks", l, b].rearrange("(r g d) n -> r d g n", r=NR, g=2)
        Vsg = g_out["Vs", l, b].rearrange("(r n) d -> r n d", r=NR)
        bs = slice(0, 1024)
        dma("sp", KsOwn[0:64, :, :], g_in["Ks", l, b].rearrange("(g d) n -> d g n", g=2)[:, :, bs], reads=[], writes=["KsOwn"])
        dma("sp", VsOwn[:, :, :], g_in["Vs", l, b][bs, :].rearrange("(m p) d -> p m d", p=128), reads=[], writes=["VsOwn"])
        for g in range(2):
            for rr in range(NR):
                dma("sp", selG[0:64, rr, :], Ksg[rr, :, g, bs], reads=[("go", "Ks", l, b)], writes=[("selG", rr)])
            allg = [("selG", rr) for rr in range(NR)]
            P.op("dve", lambda e: e.tensor_scalar(out=selacc[0:64, :], in0=selG[0:64, 0, :], scalar1=cst[0:64, C_SELA:C_SELA + 1],
                                                  scalar2=None, op0=ALU.mult), reads=allg + ["cst"], writes=["selacc"])
            for rr in range(1, NR):
                P.op("dve", lambda e, rr=rr: e.scalar_tensor_tensor(out=selacc[0:64, :], in0=selG[0:64, rr, :],
                                                                     scalar=cst[0:64, C_SELA + rr:C_SELA + rr + 1], in1=selacc[0:64, :],
                                                                     op0=ALU.mult, op1=ALU.add), reads=["selacc", ("selG", rr)], writes=["selacc"])
            for rr in range(NR):
                P.op("dve", lambda e, rr=rr: e.scalar_tensor_tensor(out=selacc[0:64, 128:1024], in0=selG[0:64, rr, 0:896],
                                                                     scalar=cst[0:64, C_SELB + rr:C_SELB + rr + 1], in1=selacc[0:64, 128:1024],
                                                                     op0=ALU.mult, op1=ALU.add), reads=["selacc", ("selG", rr)], writes=["selacc"])
            P.op("dve", lambda e, g=g: e.tensor_copy(out=KsPrev[0:64, g, :], in_=selacc[0:64, :]), reads=["selacc"], writes=[("KsPrev", g)])
        selV = selG.rearrange("p r (m d) -> p r m d", m=8)
        accV = selacc.rearrange("p (m d) -> p m d", m=8)
        for rr in range(NR):
            dma("sp", selV[:, rr, :, :], Vsg[rr, bs, :].rearrange("(m p) d -> p m d", p=128), reads=[("go", "Vs", l, b)] + [("KsPrev", 1)],
                writes=[("selG", rr)])
        allg = [("selG", rr) for rr in range(NR)]
        P.op("dve", lambda e: e.tensor_scalar(out=selacc[:, :], in0=selG[:, 0, :], scalar1=cst[:, C_SELA:C_SELA + 1],
                                              scalar2=None, op0=ALU.mult), reads=allg + ["cst"], writes=["selacc"])
        for rr in range(1, NR):
            P.op("dve", lambda e, rr=rr: e.scalar_tensor_tensor(out=selacc[:, :], in0=selG[:, rr, :],
                                                                 scalar=cst[:, C_SELA + rr:C_SELA + rr + 1], in1=selacc[:, :],
                                                                 op0=ALU.mult, op1=ALU.add), reads=["selacc", ("selG", rr)], writes=["selacc"])
        for rr in range(NR):
            P.op("dve", lambda e, rr=rr: e.scalar_tensor_tensor(out=accV[:, 1:8, :], in0=selV[:, rr, 0:7, :],
                                                                 scalar=cst[:, C_SELB + rr:C_SELB + rr + 1], in1=accV[:, 1:8, :],
                                                                 op0=ALU.mult, op1=ALU.add), reads=["selacc", ("selG", rr)], writes=["selacc"])
        P.op("dve", lambda e: e.tensor_copy(out=VsPrev[:, :, :], in_=accV), reads=["selacc"], writes=["VsPrev"])

    rot = {"k": 0, "q": 0, "s": 0, "p": 0, "o": 0}

    def nxt(key, n):
        v = rot[key] % n
        rot[key] += 1
        return v

    def attn_swa(l, tt):
        b, i = tt // 2, tt % 2
        dma("sp", qs_sb[0:64, :, :], Qs_d[l].rearrange("(h d) n -> d h n", d=64)[:, :, tt * T:(tt + 1) * T],
            reads=[("Qs", l, tt, ch) for ch in range(4)], writes=["qs_sb"])
        P.op("act", lambda e: e.activation(out=small[:, 8:12], in_=cst[:, C_SINK[l]:C_SINK[l] + 4], func=AF.Exp),
             reads=["cst"], writes=["esink"])
        for a in range(4):
            m = 4 * i + a
            for g in range(2):
                ob = 3 + nxt("o", 2)
                db = ob + 2
                plist = []
                for which in range(2):
                    Ksrc = KsPrev if which == 0 else KsOwn
                    kkey = ("KsPrev", g) if which == 0 else "KsOwn"
                    sbk = nxt("s", 3)
                    pbf = nxt("p", 4)
                    P.op("pe", lambda e, Ksrc=Ksrc, sbk=sbk, g=g, m=m, a=a: e.matmul(
                        ps[sbk][:], lhsT=Ksrc[0:64, g, m * 128:(m + 1) * 128], rhs=qs_sb[0:64, 4 * g:4 * g + 4, a * 128:(a + 1) * 128],
                        start=True, stop=True), reads=[kkey, "qs_sb"], writes=[("ps", sbk)])
                    P.op("act", lambda e, sbk=sbk, pbf=pbf: e.activation(out=Pb[:, pbf, :], in_=ps[sbk][:], func=AF.Exp),
                         reads=[("ps", sbk)], writes=[("P", pbf)])
                    if which == 0:
                        mi = M_PREV0 if m == 0 else M_SUP
                    else:
                        mi = M_TRIL

                    def mfn(e, pbf=pbf, mi=mi):
                        for j in range(4):
                            ii = e.tensor_tensor(out=Pb[:, pbf, j * 128:(j + 1) * 128], in0=Pb[:, pbf, j * 128:(j + 1) * 128],
                                                 in1=mk[:, mi, :], op=ALU.mult)
                        return ii
                    P.op("dve", mfn, reads=[("P", pbf), "mk"], writes=[("P", pbf)])
                    plist.append((which, pbf))

                def pv(e, plist=plist, g=g, m=m, ob=ob, db=db):
                    for j in range(4):
                        h = 4 * g + j
                        ro = (h % 2) * 64
                        co = (j // 2) * 128
                        for n_, (which, pbf) in enumerate(plist):
                            Vsrc = VsPrev if which == 0 else VsOwn
                            e.matmul(ps[ob][ro:ro + 64, co:co + 128], lhsT=Vsrc[:, m, g * 64:(g + 1) * 64],
                                     rhs=Pb[:, pbf, j * 128:(j + 1) * 128], start=(n_ == 0), stop=(n_ == 1))
                            ii = e.matmul(ps[db][ro:ro + 64, co:co + 128], lhsT=ones[:, 0:64],
                                          rhs=Pb[:, pbf, j * 128:(j + 1) * 128], start=(n_ == 0), stop=(n_ == 1))
                    return ii
                P.op("pe", pv, reads=[("P", p_) for _, p_ in plist] + ["VsPrev", "VsOwn", "ones"], writes=[("ps", ob), ("ps", db)])
                for pr in range(2):
                    P.op("dve", lambda e, pr=pr, g=g, db=db: e.tensor_scalar(out=tmpA[:, 0, pr * 128:(pr + 1) * 128], in0=ps[db][:, pr * 128:(pr + 1) * 128],
                                                                             scalar1=small[:, 8 + 2 * g + pr:9 + 2 * g + pr], scalar2=None, op0=ALU.add),
                         reads=[("ps", db), "esink"], writes=["tmpA0"])
                P.op("dve", lambda e: e.reciprocal(out=tmpA[:, 1, 0:256], in_=tmpA[:, 0, 0:256]), reads=["tmpA0"], writes=["tmpA1"])
                for pr in range(2):
                    P.op("dve", lambda e, pr=pr, g=g, ob=ob, a=a: e.tensor_tensor(out=hT[:, 8 + 2 * g + pr, a * 128:(a + 1) * 128],
                                                                                  in0=ps[ob][:, pr * 128:(pr + 1) * 128],
                                                                                  in1=tmpA[:, 1, pr * 128:(pr + 1) * 128], op=ALU.mult),
                         reads=[("ps", ob), "tmpA1"], writes=[("mixed", 8 + 2 * g + pr)])

    def attn_dense(l, tt, kind, h):
        b, i = tt // 2, tt % 2
        nm = 4 * i + 4
        ntok = nm * 128
        tsl = slice(tt * T, (tt + 1) * T)
        qs = nxt("q", 2)
        if kind == "mla":
            dma("sp", Qn[:, qs, :], QnT_d[l][h * 128:(h + 1) * 128, tsl], reads=[("QnT", l, tt, h)], writes=[("Qn", qs)])
            dma("sp", Qp[0:64, qs, :], QpT_d[l][h * 64:(h + 1) * 64, tsl], reads=[("QpT", l, tt, h // 2)], writes=[("Qp", qs)])
            Kg = g_out["Kn", l, b].rearrange("(r a) n -> r a n", r=NR)
            Vg = g_out["Vm", l, b].rearrange("(r n) d -> r n d", r=NR)
            dv, ro, chunk = 128, 0, h
        else:
            dma("sp", Qn[0:67, qs, :], Qf_d[l][h, :, tsl], reads=[("Qf", l, tt, h // 2), ("Qfc", l, b)], writes=[("Qn", qs)])
            Kg = g_out["Kf", l, b].rearrange("(r a) n -> r a n", r=NR)
            Vg = g_out["Vf", l, b].rearrange("(r n) d -> r n d", r=NR)
            dv, ro, chunk = 64, (h % 2) * 64, 12 + h // 2
        ob = 3 + nxt("o", 2)
        db = ob + 2
        ks_of = {}

        def load_rank(rr):
            ks = nxt("k", 3)
            if kind == "mla":
                dma("sp", Kc[:, ks, 0:ntok], Kg[rr, h * 128:(h + 1) * 128, 0:ntok],
                    reads=[("go", "Kn", l, b)], writes=[("Kc", ks)])
                dma("sp", Vc[:, ks, 0:nm, :], Vg[rr, 0:ntok, h * 128:(h + 1) * 128].rearrange("(m p) d -> p m d", p=128),
                    reads=[("go", "Vm", l, b)], writes=[("Vc", ks)])
            else:
                dma("sp", Kc[0:64, ks, 0:ntok], Kg[rr, h * 64:(h + 1) * 64, 0:ntok],
                    reads=[("go", "Kf", l, b), "Kc_ones"], writes=[("Kc", ks)])
                dma("sp", Vc[:, ks, 0:nm, 0:64], Vg[rr, 0:ntok, h * 64:(h + 1) * 64].rearrange("(m p) d -> p m d", p=128),
                    reads=[("go", "Vf", l, b), "Vc_ones"], writes=[("Vc", ks)])
            ks_of[rr] = ks
        for rr in range(3):
            load_rank(rr)
        blocks = [(rr, mp) for rr in range(NR) for mp in range(nm)]
        nblk = len(blocks)
        pend = []

        def emit_s(idx):
            rr, mp = blocks[idx]
            ks = ks_of[rr]
            a = mp - 4 * i
            c0 = max(a, 0) * 128
            sbk = nxt("s", 3)
            pbf = nxt("p", 4)
            msk = a >= 0
            if kind == "mla":
                def fn(e):
                    e.matmul(ps[sbk][:, c0:], lhsT=Kc[:, ks, mp * 128:(mp + 1) * 128], rhs=Qn[:, qs, c0:], start=True, stop=False)
                    ii = e.matmul(ps[sbk][:, c0:], lhsT=KpeAll[0:64, rr, mp * 128:(mp + 1) * 128], rhs=Qp[0:64, qs, c0:], start=False, stop=not msk)
                    if msk:
                        ii = e.matmul(ps[sbk][:, c0:c0 + 128], lhsT=mk[:, M_ID, :], rhs=mk[:, rr, :], start=False, stop=True)
                    return ii
                P.op("pe", fn, reads=[("Kc", ks), ("Qn", qs), ("Qp", qs), ("KpeAll", rr), "mk"], writes=[("ps", sbk)])
                P.op("act", lambda e: e.activation(out=Pb[:, pbf, c0:], in_=ps[sbk][:, c0:], func=AF.Exp),
                     reads=[("ps", sbk)], writes=[("P", pbf)])
            else:
                def fn(e):
                    ii = e.matmul(ps[sbk][:, c0:], lhsT=Kc[0:67, ks, mp * 128:(mp + 1) * 128], rhs=Qn[0:67, qs, c0:], start=True, stop=not msk)
                    if msk:
                        ii = e.matmul(ps[sbk][:, c0:c0 + 128], lhsT=mk[:, M_ID, :], rhs=mk[:, rr, :], start=False, stop=True)
                    return ii
                P.op("pe", fn, reads=[("Kc", ks), ("Qn", qs), "mk"], writes=[("ps", sbk)])
                P.op("act", lambda e: e.activation(out=Pb[:, pbf, c0:], in_=ps[sbk][:, c0:], func=AF.Exp,
                                                   bias=cpT[:, b, rr, mp, h:h + 1]),
                     reads=[("ps", sbk), ("cpT", b)], writes=[("P", pbf)])
            return (idx, pbf, c0)

        def emit_pv(idx, pbf, c0):
            rr, mp = blocks[idx]
            ks = ks_of[rr]
            first, last = idx == 0, idx == nblk - 1

            if kind == "mla":
                def fn(e):
                    e.matmul(ps[ob][:, c0:], lhsT=Vc[:, ks, mp, :], rhs=Pb[:, pbf, c0:], start=first, stop=last)
                    return e.matmul(ps[db][:, c0:], lhsT=ones[:, :], rhs=Pb[:, pbf, c0:], start=first, stop=last)
                P.op("pe", fn, reads=[("P", pbf), ("Vc", ks), "ones"], writes=[("ps", ob), ("ps", db)])
            else:
                P.op("pe", lambda e: e.matmul(ps[ob][:, c0:], lhsT=Vc[:, ks, mp, :], rhs=Pb[:, pbf, c0:], start=first, stop=last),
                     reads=[("P", pbf), ("Vc", ks)], writes=[("ps", ob)])
            if mp == nm - 1 and rr + 3 < NR:
                load_rank(rr + 3)

        SKEW = 2
        for idx in range(nblk):
            pend.append(emit_s(idx))
            if len(pend) > SKEW:
                emit_pv(*pend.pop(0))
        while pend:
            emit_pv(*pend.pop(0))
        if kind == "mla":
            P.op("dve", lambda e: e.reciprocal(out=rstd[:, :], in_=ps[db][:, :]), reads=[("ps", db)], writes=["rstd"])
            P.op("dve", lambda e: e.tensor_tensor(out=hT[:, chunk, :], in0=ps[ob][:, :], in1=rstd[:, :], op=ALU.mult),
                 reads=[("ps", ob), "rstd"], writes=[("mixed", chunk, ro)])
        else:
            P.op("dve", lambda e: e.reciprocal(out=rstd[0:64, :], in_=ps[ob][64:128, :]), reads=[("ps", ob)], writes=["rstd"])
            if ro == 0:
                P.op("dve", lambda e: e.tensor_tensor(out=hT[0:64, chunk, :], in0=ps[ob][0:64, :], in1=rstd[0:64, :], op=ALU.mult),
                     reads=[("ps", ob), "rstd"], writes=[("mixed", chunk, ro)])
            else:
                P.op("dve", lambda e: e.tensor_tensor(out=tmpA[0:64, 0, :], in0=ps[ob][0:64, :], in1=rstd[0:64, :], op=ALU.mult),
                     reads=[("ps", ob), "rstd"], writes=["tmpA0"])
                P.op("dve", lambda e: e.tensor_copy(out=hT[64:128, chunk, :], in_=tmpA[0:64, 0, :]),
                     reads=["tmpA0"], writes=[("mixed", chunk, ro)])

    def attention_tile(l, tt):
        b, i = tt // 2, tt % 2
        nm = 4 * i + 4
        ntok = nm * 128
        if i == 0:
            P.barrier()
            fox_c_prep(l, b)
            P.barrier()
            swa_select(l, b)
            P.barrier()
        attn_swa(l, tt)
        Kpg = g_out["Kpe", l, b].rearrange("(r a) n -> r a n", r=NR)
        for rr in range(NR):
            dma("sp", KpeAll[0:64, rr, 0:ntok], Kpg[rr, :, 0:ntok], reads=[("go", "Kpe", l, b)], writes=[("KpeAll", rr)])
        for h in range(8):
            attn_dense(l, tt, "mla", h)
        P.op("dve", lambda e: e.memset(Kc[64:67, :, :], 1.0), reads=[("Kc", 0), ("Kc", 1), ("Kc", 2)], writes=["Kc_ones", ("Kc", 0), ("Kc", 1), ("Kc", 2)])
        P.op("dve", lambda e: e.memset(Vc[:, :, :, 64:128], 1.0), reads=[("Vc", 0), ("Vc", 1), ("Vc", 2)], writes=["Vc_ones", ("Vc", 0), ("Vc", 1), ("Vc", 2)])
        for h in range(8):
            attn_dense(l, tt, "fox", h)

    def wout_and_residual(l, tt):
        xres_v = xres.rearrange("(k p) n -> p k n", p=128)
        dma("sp", xT[:], xres_v[:, :, tt * T:(tt + 1) * T], reads=[("xres", tt)], writes=["xT"])
        wo_d = w_b["wout", l].rearrange("(k p) c -> p k c", p=128)
        mixed_all = [("mixed", c_) for c_ in range(8, 12)] + [("mixed", c_, 0) for c_ in range(8)] + \
                    [("mixed", c_, r_) for c_ in range(12, 16) for r_ in (0, 64)]
        if debug and l == 0:
            dma("sp", dbg_mixed.rearrange("(k p) n -> p k n", p=128)[:, :, tt * T:(tt + 1) * T], hT[:], reads=mixed_all, writes=[("dbgm", tt)])
        for blk in range(8):
            s = wblock(wo_d, blk * 256, 256, ("wb", "wout", l), ("wout", l))
            for cc in range(2):
                c = blk * 2 + cc
                pb = c % 2

                def fn(e, s=s, cc=cc, pb=pb):
                    for k in range(KC):
                        ii = e.matmul(ps[pb][:], lhsT=wgu[:, s, k, cc * 128:(cc + 1) * 128], rhs=hT[:, k, :], start=(k == 0), stop=(k == KC - 1))
                    return ii
                P.op("pe", fn, reads=[("wgu", s)] + mixed_all, writes=[("ps", pb)])
                P.op("dve", lambda e, c=c, pb=pb: e.tensor_tensor(out=xT[:, c, :], in0=ps[pb][:], in1=xT[:, c, :], op=ALU.add),
                     reads=[("ps", pb), "xT"], writes=["xT"])
        if debug and l == 0:
            dma("sp", dbg_x.rearrange("(k p) n -> p k n", p=128)[:, :, tt * T:(tt + 1) * T], xT[:], reads=["xT"], writes=[("dbgx", tt)])

    L0 = ["wg1", "wu1", "wd1", "win", "winsw", "wqb", "wqbsw", "wkvb", "wout", "wg2", "wu2", "wd2"]
    for nm in L0:
        conv_weight(nm, 0)
    xin_v = xT_in.rearrange("(k p) n -> p k n", p=128)
    out_v = outT.rearrange("(k p) n -> p k n", p=128)
    for tt in range(NT):
        dma("sp", xT[:], xin_v[:, :, tt * T:(tt + 1) * T], writes=["xT"])
        phase_a_tile(tt, 0, next_ffn=(0, 0) if tt + 1 < NT else None)
        if tt % 2 == 1:
            gather_all(0, tt // 2)
    out_ops = []
    for l in range(nlayers):
        P.barrier()
        if l + 1 < nlayers:
            for nm in L0:
                conv_weight(nm, l + 1)
        for tt in range(NT):
            P.barrier()
            attention_tile(l, tt)
            prefetch_wout(l)
            P.barrier()
            wout_and_residual(l, tt)
            prefetch_ffn(1, l)
            P.barrier()
            rmsnorm_to_hT(C_FFN2N[l])
            ffn(1, l)
            if l + 1 < nlayers:
                phase_a_tile(tt, l + 1)
                if tt % 2 == 1:
                    gather_all(l + 1, tt // 2)
            else:
                if nlayers == DEPTH:
                    rmsnorm_to_hT(C_FINN, out_f32=True)
                out_ops.append(dma("sp", out_v[:, :, tt * T:(tt + 1) * T], xT[:], reads=["xT"], writes=[("out", tt)]))
    P.barrier()
    assert not pref, pref
    P.emit(st)
    st.close()
    return nc


def _fm(v):
    return np.ascontiguousarray(v.reshape(-1, 128).T)


def make_consts(inp, r):
    c = np.zeros((128, NCONST), np.float32)
    for l in range(DEPTH):
        c[:, C_FFN1N[l]:C_FFN1N[l] + KC] = _fm(inp["ffn1_norm"][l])
        c[:, C_MIXN[l]:C_MIXN[l] + KC] = _fm(inp["mix_norm"][l])
        c[:, C_FFN2N[l]:C_FFN2N[l] + KC] = _fm(inp["ffn2_norm"][l])
        c[:, C_QN[l]:C_QN[l] + 4] = _fm(inp["mla_q_norm"][l])
        c[:, C_KVN[l]:C_KVN[l] + 2] = _fm(inp["mla_kv_norm"][l])
        sk = inp["swa_sinks"][l]
        for pr in range(4):
            c[0:64, C_SINK[l] + pr] = sk[2 * pr]
            c[64:128, C_SINK[l] + pr] = sk[2 * pr + 1]
        c[0:8, C_FB[l]] = inp["fox_forget_bias"][l]
    c[:, C_FINN:C_FINN + KC] = _fm(inp["final_norm"])
    inv = (np.float32(10000.0) ** (-(np.arange(0, 64, 2, dtype=np.float32)) / np.float32(64))).astype(np.float32)
    p = np.arange(128)
    c[:, C_INVF] = inv[p % 32]
    c[:, C_SGN] = np.where((p % 64) < 32, -1.0, 1.0)
    for rr in range(8):
        c[:, C_OH + rr] = 1.0 if rr == r else 0.0
        c[:, C_SELA + rr] = 1.0 if (r >= 1 and rr == r - 1) else 0.0
        c[:, C_SELB + rr] = 1.0 if (r == 0 and rr == 7) else 0.0
    c[0:8, C_ID8:C_ID8 + 8] = np.eye(8, dtype=np.float32)
    return c


def make_masks(r):
    pk = np.arange(128)[:, None]
    pq = np.arange(128)[None, :]
    tril = (pk <= pq).astype(np.float32)
    sup = (pk > pq).astype(np.float32)
    m = np.zeros((128, NMASK, 128), np.float32)
    for rr in range(8):
        if rr < r:
            m[:, rr] = 0.0
        elif rr == r:
            m[:, rr] = NEG * sup
        else:
            m[:, rr] = NEG
    m[:, M_ID] = np.eye(128, dtype=np.float32)
    m[:, M_TRIL] = tril
    m[:, M_SUP] = sup
    m[:, M_PREV0] = 0.0 if r == 0 else sup
    return np.ascontiguousarray(m.reshape(128, NMASK * 128)).astype(ml_dtypes.bfloat16)


def shard_tokens(a, r):
    sh = a.shape
    v = a.reshape((2, 8, 8, 128) + sh[2:])[:, :, r]
    return v.reshape((NTOK,) + sh[2:])


def unshard_out(outs, dtype):
    full = np.zeros((2, S, D), dtype)
    fv = full.reshape(2, 8, 8, 128, D)
    for r, o in enumerate(outs):
        fv[:, :, r] = o.T.reshape(2, 8, 128, D)
    return full


def _swap_halves(w, starts):
    cols = []
    for s0 in starts:
        cols.extend(range(s0 + 32, s0 + 64))
        cols.extend(range(s0, s0 + 32))
    return w[..., cols]


def prep_weights(inp):
    W = {}
    W["wg1"], W["wu1"], W["wd1"] = inp["ffn1_w_gate"], inp["ffn1_w_up"], inp["ffn1_w_down"]
    W["wg2"], W["wu2"], W["wd2"] = inp["ffn2_w_gate"], inp["ffn2_w_up"], inp["ffn2_w_down"]
    W["win"] = inp["w_in"]
    W["winsw"] = _swap_halves(inp["w_in"], [O_KR] + [O_QS + 64 * h for h in range(8)] + [O_KS, O_KS + 64])
    qb = inp["mla_w_q_b"]
    nope = [h * 192 + j for h in range(8) for j in range(128)]
    pe = [h * 192 + 128 + j for h in range(8) for j in range(64)]
    W["wqb"] = qb[..., nope + pe]
    W["wqbsw"] = _swap_halves(qb, [h * 192 + 128 for h in range(8)])
    kvb = inp["mla_w_kv_b"]
    kn = [h * 256 + j for h in range(8) for j in range(128)]
    vv = [h * 256 + 128 + j for h in range(8) for j in range(128)]
    W["wkvb"] = kvb[..., kn + vv]
    W["wout"] = inp["w_out"]
    return W


_NC_CACHE = {}


def make_in_maps(inp):
    W = prep_weights(inp)
    in_maps = []
    for r in range(8):
        m = {"xT": np.ascontiguousarray(shard_tokens(inp["x"], r).T),
             "consts": make_consts(inp, r), "masks": make_masks(r),
             "pos": np.ascontiguousarray(shard_tokens(inp["positions"], r).reshape(1, NTOK).astype(np.int32))}
        for nm, w in W.items():
            R = w.shape[1]
            m[nm] = np.ascontiguousarray(w[:, r * (R // 8):(r + 1) * (R // 8), :])
        in_maps.append(m)
    return in_maps


def kernel(**inputs):
    inp = {k: np.asarray(v) for k, v in inputs.items()}
    if "all" not in _NC_CACHE:
        _NC_CACHE["all"] = build_program(debug=True)
    nc = _NC_CACHE["all"]
    in_maps = make_in_maps(inp)
    res = run_bass_kernel_spmd(nc, in_maps, core_ids=list(range(8)))
    outs = [np.asarray(r["outT"]) for r in res.results]
    return unshard_out(outs, np.float32)
```
